# Optimizing a Trainium2 kernel written in Bass

```python
import math
import jax, jax.numpy as jnp
from jax import lax
import numpy as np

D_MODEL = 1024
BATCH = 32
SEQ = 2048
DEPTH = 1

HEAD_DIM = 64
N_HEADS = D_MODEL // HEAD_DIM
N_HEADS_FOX = N_HEADS // 2
N_HEADS_DIL = N_HEADS - N_HEADS_FOX
W_FOX = N_HEADS_FOX * HEAD_DIM
W_DIL = N_HEADS_DIL * HEAD_DIM
DILATION_PAIRS = ((128, 1), (512, 4), (2048, 16))
ROPE_THETA = 500000.0
ROPE_DIM = HEAD_DIM // 4
Q_BLOCK = 128
D_FF = -(-8 * D_MODEL // (3 * 256)) * 256
EPS = 1e-6
NEG = -1e30
IN_SPLITS = (W_FOX, 2 * W_FOX, 3 * W_FOX, 3 * W_FOX + N_HEADS_FOX,
             3 * W_FOX + N_HEADS_FOX + W_DIL, 3 * W_FOX + N_HEADS_FOX + 2 * W_DIL)
IN_COLS = 3 * W_FOX + N_HEADS_FOX + 3 * W_DIL

kernel_name = "hymba_fox_dilated_hybrid"


def rms_norm(x, g):
    xf = x.astype(jnp.float32)
    y = xf * lax.rsqrt(jnp.mean(xf * xf, axis=-1, keepdims=True) + EPS)
    return (y * g.astype(jnp.float32)).astype(x.dtype)


def partial_rope(x, pos):
    half = ROPE_DIM // 2
    inv_freq = jnp.power(jnp.float32(ROPE_THETA),
                         -jnp.arange(half, dtype=jnp.float32) * 2.0 / ROPE_DIM)
    ang = pos.astype(jnp.float32)[:, None] * inv_freq[None, :]
    cos = jnp.cos(ang)[None, :, None, :]
    sin = jnp.sin(ang)[None, :, None, :]
    x1 = x[..., :half]
    x2 = x[..., half:ROPE_DIM]
    return jnp.concatenate([x1 * cos - x2 * sin, x2 * cos + x1 * sin, x[..., ROPE_DIM:]], axis=-1)


def fox_attention(q, k, v, log_f):
    S = q.shape[1]
    c = jnp.transpose(jnp.cumsum(log_f, axis=1), (0, 2, 1))
    scale = HEAD_DIM ** -0.5
    outs = []
    for i in range(S // Q_BLOCK):
        q0, q1 = i * Q_BLOCK, (i + 1) * Q_BLOCK
        s = jnp.einsum('bqhe,bkhe->bhqk', q[:, q0:q1], k[:, :q1]) * scale
        s = s + (c[:, :, q0:q1, None] - c[:, :, None, :q1])
        mask = np.arange(q0, q1)[:, None] >= np.arange(q1)[None, :]
        s = jnp.where(mask[None, None], s, NEG)
        p = jax.nn.softmax(s, axis=-1)
        outs.append(jnp.einsum('bhqk,bkhe->bqhe', p, v[:, :q1]))
    return jnp.concatenate(outs, axis=1)


def dilated_branch(q, k, v, window, dilation):
    B, S, H, D = q.shape
    n = S // dilation
    wk = window // dilation
    pad = min(wk, n)
    bq = math.gcd(Q_BLOCK, n)
    nb = n // bq
    L = bq + pad
    scale = HEAD_DIM ** -0.5
    qs = q.reshape(B, nb, bq, dilation, H, D)
    kp = jnp.pad(k.reshape(B, n, dilation, H, D), ((0, 0), (pad, 0), (0, 0), (0, 0), (0, 0)))
    vp = jnp.pad(v.reshape(B, n, dilation, H, D), ((0, 0), (pad, 0), (0, 0), (0, 0), (0, 0)))
    starts = np.arange(nb) * bq
    idx = starts[:, None] + np.arange(L)[None, :]
    kb = kp[:, idx]
    vb = vp[:, idx]
    s = jnp.einsum('bnirhe,bnjrhe->bnrhij', qs, kb) * scale
    ii = np.arange(bq)[:, None]
    jj = np.arange(L)[None, :]
    dist = ii + pad - jj
    key_pos = starts[:, None, None] + jj[None] - pad
    mask = (dist >= 0)[None] & (dist <= wk)[None] & (key_pos >= 0)
    s = jnp.where(mask[None, :, None, None], s, NEG)
    lse = jax.nn.logsumexp(s, axis=-1)
    p = jnp.exp(s - lse[..., None])
    o = jnp.einsum('bnrhij,bnjrhe->bnirhe', p, vb).reshape(B, S, H, D)
    lse = jnp.transpose(lse, (0, 1, 4, 2, 3)).reshape(B, S, H)
    return o, lse


def dilated_attention(q, k, v):
    outs, lses = [], []
    for window, dilation in DILATION_PAIRS:
        o, l = dilated_branch(q, k, v, window, dilation)
        outs.append(o)
        lses.append(l)
    w = jax.nn.softmax(jnp.stack(lses, axis=0), axis=0)
    return jnp.sum(w[..., None] * jnp.stack(outs, axis=0), axis=0)


def setup_inputs(seed: int = 0) -> dict:
    key = jax.random.key(seed)
    ks = jax.random.split(key, 16)
    f32 = jnp.float32

    def gain(k, shape):
        return jnp.ones(shape, f32) + 0.02 * jax.random.normal(k, shape, f32)

    return {
        "x": jax.random.normal(ks[0], (BATCH, SEQ, D_MODEL), f32),
        "g_mix": gain(ks[1], (DEPTH, D_MODEL)),
        "w_in": jax.random.normal(ks[2], (DEPTH, D_MODEL, IN_COLS), f32) * D_MODEL ** -0.5,
        "b_forget": jax.random.uniform(ks[3], (DEPTH, N_HEADS_FOX), f32, minval=1.0, maxval=4.0),
        "g_q_fox": gain(ks[4], (DEPTH, HEAD_DIM)),
        "g_k_fox": gain(ks[5], (DEPTH, HEAD_DIM)),
        "g_q_dil": gain(ks[6], (DEPTH, HEAD_DIM)),
        "g_k_dil": gain(ks[7], (DEPTH, HEAD_DIM)),
        "g_out_fox": gain(ks[8], (DEPTH, W_FOX)),
        "g_out_dil": gain(ks[9], (DEPTH, W_DIL)),
        "w_out": jax.random.normal(ks[10], (DEPTH, D_MODEL, D_MODEL), f32) * D_MODEL ** -0.5,
        "g_ffn": gain(ks[11], (DEPTH, D_MODEL)),
        "w_gate": jax.random.normal(ks[12], (DEPTH, D_MODEL, D_FF), f32) * D_MODEL ** -0.5,
        "w_up": jax.random.normal(ks[13], (DEPTH, D_MODEL, D_FF), f32) * D_MODEL ** -0.5,
        "w_down": jax.random.normal(ks[14], (DEPTH, D_FF, D_MODEL), f32) * D_FF ** -0.5,
    }


def reference(x, g_mix, w_in, b_forget, g_q_fox, g_k_fox, g_q_dil, g_k_dil,
              g_out_fox, g_out_dil, w_out, g_ffn, w_gate, w_up, w_down):
    B, S, _ = x.shape
    f32 = jnp.float32
    pos = jnp.arange(S)

    def heads(t, n_h):
        return t.reshape(B, S, n_h, HEAD_DIM).astype(f32)

    for l in range(DEPTH):
        h = rms_norm(x, g_mix[l])
        proj = jnp.einsum('bsd,dc->bsc', h, w_in[l])
        qa, ka, va, fa, qd, kd, vd = jnp.split(proj, IN_SPLITS, axis=-1)

        qa = rms_norm(heads(qa, N_HEADS_FOX), g_q_fox[l])
        ka = rms_norm(heads(ka, N_HEADS_FOX), g_k_fox[l])
        va = heads(va, N_HEADS_FOX)
        log_f = jax.nn.log_sigmoid(fa.astype(f32) + b_forget[l].astype(f32))
        o_fox = fox_attention(qa, ka, va, log_f).reshape(B, S, W_FOX)

        qd = partial_rope(rms_norm(heads(qd, N_HEADS_DIL), g_q_dil[l]), pos)
        kd = partial_rope(rms_norm(heads(kd, N_HEADS_DIL), g_k_dil[l]), pos)
        vd = heads(vd, N_HEADS_DIL)
        o_dil = dilated_attention(qd, kd, vd).reshape(B, S, W_DIL)

        o = jnp.concatenate([rms_norm(o_fox, g_out_fox[l]), rms_norm(o_dil, g_out_dil[l])], axis=-1)
        x = x + jnp.einsum('bsc,cd->bsd', o.astype(x.dtype), w_out[l])

        h = rms_norm(x, g_ffn[l])
        a = jnp.einsum('bsd,df->bsf', h, w_gate[l])
        u = jnp.einsum('bsd,df->bsf', h, w_up[l])
        x = x + jnp.einsum('bsf,fd->bsd', jax.nn.silu(a) * u, w_down[l])
    return x
```

```python
import contextlib
import math
import numpy as np
import ml_dtypes
import concourse.bass as bass
import concourse.mybir as mybir
from concourse.bass_utils import run_bass_kernel_spmd

F32 = mybir.dt.float32
BF16 = mybir.dt.bfloat16
ALU = mybir.AluOpType
AF = mybir.ActivationFunctionType
AX = mybir.AxisListType

D = 1024
KC = 8
S = 2048
NT = 16
HD = 64
DFF = 2816
NFC = 22
INC = 3080
EPS = 1e-6
NCORES = 8
ROPE_THETA = 500000.0
ROPE_ENG = "pool"
D4_ENG = "act"
PIPE_LOOK = 2
HEAD_DMA_Q = "act"
SIDE_FRAC = 0.8
FOX_K = 66
BOUNCE_Q = "sp"
PREP_STRIDE = 2
PREP_SPLIT = False
VCOPY_ENG = "dve"
DIL_MASK_ENG = lambda pi: "dve"
SIDE_FIRST = False
PIPE_DEFER = 3


class _Stop(Exception):
    pass


class Reg:
    __slots__ = ("name", "lw", "rd", "excl")

    def __init__(self, name):
        self.name = name
        self.lw = None
        self.rd = []
        self.excl = False


class Op:
    __slots__ = ("eng", "fn", "deps", "dma", "sem", "val", "signal", "cnt", "waits", "idx")

    def __init__(self, eng, fn, deps, dma, idx):
        self.eng = eng
        self.fn = fn
        self.deps = deps
        self.dma = dma
        self.sem = None
        self.val = 0
        self.signal = False
        self.cnt = 0
        self.waits = []
        self.idx = idx


ENGS = ("pe", "act", "dve", "pool", "sp")
BLOCKNAME = {"pe": "tensor", "act": "scalar", "dve": "vector", "pool": "gpsimd", "sp": "sync"}


class Prog:
    def __init__(self, nc, es, dma_pool=8):
        self.nc = nc
        self.es = es
        self.ops = []
        self.dma_pool = dma_pool
        self.nreg = 0

    def sb(self, name, shape, dt):
        return self.es.enter_context(self.nc.sbuf_tensor(name, list(shape), dt))

    def ps(self, name, shape, dt):
        return self.es.enter_context(self.nc.psum_tensor(name, list(shape), dt))

    def reg(self, name=None):
        self.nreg += 1
        return Reg(name or f"r{self.nreg}")

    def regs(self, n, name="r"):
        return [self.reg(f"{name}{i}") for i in range(n)]

    max_ops = None

    def op(self, eng, fn, r=(), w=(), dma=False):
        i = len(self.ops)
        if Prog.max_ops is not None and i >= Prog.max_ops:
            raise _Stop()
        if any(g.excl for g in r):
            w = list(w) + [g for g in r if g.excl]
            r = [g for g in r if not g.excl]
        deps = {}
        for g in r:
            if g.lw is not None:
                deps[g.lw] = True
        for g in w:
            if g.lw is not None:
                deps.setdefault(g.lw, False)
            for j in g.rd:
                deps.setdefault(j, False)
        for g in r:
            g.rd.append(i)
        for g in w:
            g.lw = i
            g.rd = []
        self.ops.append(Op(eng, fn, deps, dma, i))
        return i

    def handoff(self, old, new):
        allops = set()
        for g in old:
            if g.lw is not None:
                allops.add(g.lw)
            allops.update(g.rd)
        lst = sorted(allops)
        for g in new:
            g.rd = list(set(g.rd) | set(lst))

    def dma(self, q, out, in_, r=(), w=()):
        return self.op(q, lambda e: e.dma_start(out=out, in_=in_), r=r, w=w, dma=True)

    def mm(self, out, lhsT, rhs, start=True, stop=True, r=(), w=()):
        return self.op("pe", lambda e: e.matmul(out, lhsT=lhsT, rhs=rhs, start=start, stop=stop), r=r, w=w)

    def act(self, out, in_, func, r=(), w=(), **kw):
        return self.op("act", lambda e: e.activation(out=out, in_=in_, func=func, **kw), r=r, w=w)

    def tt(self, eng, out, in0, in1, op, r=(), w=()):
        return self.op(eng, lambda e: e.tensor_tensor(out=out, in0=in0, in1=in1, op=op), r=r, w=w)

    def stt(self, eng, out, in0, scalar, in1, op0, op1, r=(), w=()):
        return self.op(eng, lambda e: e.scalar_tensor_tensor(out=out, in0=in0, scalar=scalar, in1=in1, op0=op0, op1=op1),
                       r=r, w=w)

    def cp(self, eng, out, in_, r=(), w=()):
        if eng == "act":
            return self.op("act", lambda e: e.activation(out=out, in_=in_, func=AF.Copy), r=r, w=w)
        return self.op(eng, lambda e: e.tensor_copy(out=out, in_=in_), r=r, w=w)

    def finish(self):
        nc = self.nc
        ops = self.ops
        LIM = 16000
        sem_eng_l = {e: [] for e in ENGS}

        def sem_for(e, cnt):
            ep = (cnt - 1) // LIM
            while len(sem_eng_l[e]) <= ep:
                sem_eng_l[e].append(self.es.enter_context(nc.semaphore(f"s_{e}{len(sem_eng_l[e])}")))
            return sem_eng_l[e][ep], (cnt - 1) % LIM + 1
        dma_sems = {e: [self.es.enter_context(nc.semaphore(f"d_{e}{k}")) for k in range(self.dma_pool)]
                    for e in ("sp", "act", "pool")}
        dcount = {e: 0 for e in dma_sems}
        prev_on_sem = {}
        for o in ops:
            if o.dma:
                k = dcount[o.eng]
                dcount[o.eng] += 1
                slot = k % self.dma_pool
                o.sem = dma_sems[o.eng][slot]
                o.val = 16 * (k // self.dma_pool + 1)
                key = (o.eng, slot)
                if key in prev_on_sem:
                    o.deps.setdefault(prev_on_sem[key], False)
                prev_on_sem[key] = o.idx
        seen_eng = {e: {p: -1 for p in ENGS} for e in ENGS}
        seen_dma = {e: {} for e in ENGS}
        need = []
        for o in ops:
            e = o.eng
            best = {}
            dwaits = {}
            for d, is_raw in o.deps.items():
                Dp = ops[d]
                if Dp.dma:
                    key = id(Dp.sem)
                    if seen_dma[e].get(key, 0) >= Dp.val:
                        continue
                    if key not in dwaits or dwaits[key][1] < Dp.val:
                        dwaits[key] = (Dp.sem, Dp.val)
                else:
                    p = Dp.eng
                    if p == e and (e == "pe" or not is_raw):
                        continue
                    if d <= seen_eng[e][p]:
                        continue
                    if p not in best or best[p] < d:
                        best[p] = d
            o.waits = []
            for key, (sem, val) in dwaits.items():
                seen_dma[e][key] = val
                o.waits.append((sem, val))
            prods = []
            for p, d in best.items():
                seen_eng[e][p] = d
                ops[d].signal = True
                prods.append(d)
            need.append(prods)
        run = {e: 0 for e in ENGS}
        for o in ops:
            if o.dma:
                continue
            if o.signal:
                run[o.eng] += 1
            o.cnt = run[o.eng]
        for o, prods in zip(ops, need):
            for d in prods:
                Dp = ops[d]
                o.waits.append(sem_for(Dp.eng, Dp.cnt))
        final_waits = []
        for key, idx in prev_on_sem.items():
            Dp = ops[idx]
            final_waits.append((Dp.sem, Dp.val))
        by_eng = {e: [o for o in ops if o.eng == e] for e in ENGS}
        self.stats = {e: len(by_eng[e]) for e in ENGS}
        self.stats["signals"] = sum(1 for o in ops if o.signal)
        self.stats["waits"] = sum(len(o.waits) for o in ops)
        with nc.Block() as block:
            for e in ENGS:
                lst = by_eng[e]

                def f(eng, lst=lst, e=e):
                    for o in lst:
                        for sem, val in o.waits:
                            eng.wait_ge(sem, val)
                        ins = o.fn(eng)
                        if o.dma:
                            ins.then_inc(o.sem, 16)
                        elif o.signal:
                            ins.then_inc(sem_for(e, o.cnt)[0], 1)
                    if e == "sp":
                        for sem, val in final_waits:
                            eng.wait_ge(sem, val)

                getattr(block, BLOCKNAME[e])(f)


def _v(ap, shape):
    if len(shape) == 1:
        return ap
    if len(shape) == 2:
        return ap.rearrange("p (a b) -> p a b", a=shape[0])
    if len(shape) == 3:
        return ap.rearrange("p (a b c) -> p a b c", a=shape[0], b=shape[1])
    raise ValueError


def build(nseq=4, dbg=None, stop_after=None):
    nc = bass.Bass("TRN2", target_bir_lowering=False)
    build.dumps = []

    def chk(stage):
        if stop_after == stage:
            raise _Stop()

    def din(name, shape, dt=F32):
        return nc.dram_tensor(name, list(shape), dt, kind="ExternalInput").ap()

    def dscr(name, shape, dt=BF16):
        return nc.dram_tensor(name, list(shape), dt, kind="Internal").ap()

    x = din("x", [nseq, S, D])
    g_mix = din("g_mix", [1, D])
    w_in = din("w_in", [D, INC])
    b_forget = din("b_forget", [1, 8])
    g_q_fox = din("g_q_fox", [1, HD])
    g_k_fox = din("g_k_fox", [1, HD])
    g_q_dil = din("g_q_dil", [1, HD])
    g_k_dil = din("g_k_dil", [1, HD])
    g_out_fox = din("g_out_fox", [1, 512])
    g_out_dil = din("g_out_dil", [1, 512])
    w_out = din("w_out", [D, D])
    g_ffn = din("g_ffn", [1, D])
    w_gate = din("w_gate", [D, DFF])
    w_up = din("w_up", [D, DFF])
    w_down = din("w_down", [DFF, D])
    c_ident = din("c_ident", [128, 128], BF16)
    c_perm4 = din("c_perm4", [128, 128], BF16)
    c_perm16 = din("c_perm16", [128, 128], BF16)
    c_tri = din("c_tri", [128, 128])
    c_sel = din("c_sel", [128, 128])
    c_cos = din("c_cos", [128, NT * 8])
    c_sin = din("c_sin", [128, NT * 8])
    c_mcb = din("c_mcb", [128, 256], BF16)
    c_mc4 = din("c_mc4", [128, 512], BF16)
    c_mbc = din("c_mbc", [128, 256], BF16)
    c_m16 = din("c_m16", [128, 4 * 512], BF16)
    out = nc.dram_tensor("out", [nseq, S, D], F32, kind="ExternalOutput").ap()
    dbg_out = {}

    winb = dscr("winb", [D, 8, 384])
    wfb = dscr("wfb", [D, 8])
    woutb = dscr("woutb", [D, D])
    wgb = dscr("wgb", [D, DFF])
    wub = dscr("wub", [D, DFF])
    wdb = dscr("wdb", [DFF, D])
    vscr = [dscr(f"vscr{i}", [S, 130]) for i in range(2)]

    with contextlib.ExitStack() as es:
        P = Prog(nc, es)
        Rdump = P.reg("dump")

        def dump(name, ap, r):
            if not dbg or name not in dbg:
                return
            t_ = nc.dram_tensor(name, list(ap.shape), ap.dtype, kind="ExternalOutput").ap()
            build.dumps.append(name)
            P.dma("sp", t_, ap, r=r, w=[Rdump])
        ident = P.sb("ident", [128, 128], BF16)
        tri = P.sb("tri", [128, 128], F32)
        sel = P.sb("sel", [128, 128], F32)
        cos_t = P.sb("cos_t", [128, NT, 8], F32)
        sin_t = P.sb("sin_t", [128, NT, 8], F32)
        mcb = P.sb("mcb", [128, 256], BF16)
        mbc = P.sb("mbc", [128, 256], BF16)
        mC4 = P.sb("mC4", [128, 512], BF16)
        mCB2 = P.sb("mCB2", [128, 512], BF16)
        mBC2 = P.sb("mBC2", [128, 512], BF16)
        zeros_t = P.sb("zeros_t", [128, 512], BF16)
        m16 = P.sb("m16", [128, 4, 512], BF16)
        gmix_b = P.sb("gmix_b", [128, D], F32)
        gffn_b = P.sb("gffn_b", [128, D], F32)
        gout_b = P.sb("gout_b", [128, D], F32)
        gd_b = P.sb("gd_b", [128, 4, HD], F32)
        gfc = P.sb("gfc", [128, 2], F32)
        gs_f = P.sb("gs_f", [128, 1], F32)
        bf_b = P.sb("bf_b", [128, 8], F32)
        Rc = P.reg("consts")
        Rcl = []

        def cnew():
            Rcl.append(P.reg())
            return Rcl[-1]
        P.dma("sp", ident[:], c_ident, w=[cnew()])
        P.dma("sp", tri[:], c_tri, w=[cnew()])
        P.dma("sp", sel[:], c_sel, w=[cnew()])
        P.dma("sp", cos_t[:].rearrange("p a b -> p (a b)"), c_cos, w=[cnew()])
        P.dma("sp", sin_t[:].rearrange("p a b -> p (a b)"), c_sin, w=[cnew()])
        P.dma("sp", mcb[:], c_mcb, w=[cnew()])
        P.dma("sp", mbc[:], c_mbc, w=[cnew()])
        for h_ in range(2):
            P.dma("sp", mC4[:, h_ * 128:(h_ + 1) * 128], c_mcb[:, 0:128], w=[cnew()])
            P.dma("sp", mC4[:, (h_ + 2) * 128:(h_ + 3) * 128], c_mcb[:, 0:128], w=[cnew()])
            P.dma("sp", mCB2[:, h_ * 256:(h_ + 1) * 256], c_mcb, w=[cnew()])
            P.dma("sp", mBC2[:, h_ * 256:(h_ + 1) * 256], c_mbc, w=[cnew()])
        P.dma("sp", m16[:].rearrange("p a b -> p (a b)"), c_m16, w=[cnew()])
        P.dma("sp", gmix_b[:], g_mix.partition_broadcast(128), w=[cnew()])
        P.dma("sp", gffn_b[:], g_ffn.partition_broadcast(128), w=[cnew()])
        P.dma("sp", gout_b[:, 0:512], g_out_fox.partition_broadcast(128), w=[cnew()])
        P.dma("sp", gout_b[:, 512:1024], g_out_dil.partition_broadcast(128), w=[cnew()])
        P.dma("sp", gd_b[:, 0, :], g_q_dil.partition_broadcast(128), w=[cnew()])
        P.dma("sp", gd_b[:, 2, :], g_k_dil.partition_broadcast(128), w=[cnew()])
        P.dma("sp", bf_b[:], b_forget.partition_broadcast(128), w=[cnew()])
        P.dma("sp", gfc[0:64, 0:1], g_q_fox.rearrange("o e -> e o"), w=[cnew()])
        P.dma("sp", gfc[0:64, 1:2], g_k_fox.rearrange("o e -> e o"), w=[cnew()])
        P.op("dve", lambda e: e.tensor_copy(out=gd_b[:, 1, :], in_=gd_b[:, 0, :]), r=Rcl, w=[Rc])
        P.op("dve", lambda e: e.tensor_scalar_mul(out=gd_b[:, 2, :], in0=gd_b[:, 2, :], scalar1=8.0),
             r=[Rc], w=[Rc])
        P.op("dve", lambda e: e.tensor_copy(out=gd_b[:, 3, :], in_=gd_b[:, 2, :]), r=[Rc], w=[Rc])
        P.op("dve", lambda e: e.memset(gs_f[:], 1.0), w=[Rc])
        P.op("dve", lambda e: e.memset(zeros_t[:], 0.0), w=[Rc])
        P.stt("dve", gs_f[0:64, :], gfc[0:64, 0:1], 8.0, gfc[0:64, 1:2], ALU.mult, ALU.mult, r=[Rc], w=[Rc])

        Rwin = P.regs(8, "winb")
        Rwf = P.reg("wfb")
        Rwo, Rwg, Rwu, Rwd = P.reg("wo"), P.reg("wg"), P.reg("wu"), P.reg("wd")
        def job_cols(j):
            p = j // 2
            if j % 2 == 0:
                return [1544 + 128 * p, 2056 + 128 * p, 2568 + 128 * p]
            return [128 * p, 512 + 128 * p, 1024 + 128 * p]
        P.dma("pool", wfb, w_in[:, 1536:1544], w=[Rwf])
        for j in range(8):
            for part, c0 in enumerate(job_cols(j)):
                P.dma("pool", winb[:, j, part * 128:(part + 1) * 128], w_in[:, c0:c0 + 128], w=[Rwin[j]])
        pending_casts = []
        for i in range(KC):
            rs_ = slice(i * 128, (i + 1) * 128)
            pending_casts.append((woutb[rs_, :], w_out[rs_, :], Rwo))
        for i in range(KC):
            rs_ = slice(i * 128, (i + 1) * 128)
            pending_casts.append((wgb[rs_, :], w_gate[rs_, :], Rwg))
            pending_casts.append((wub[rs_, :], w_up[rs_, :], Rwu))
        for i in range(NFC):
            rs_ = slice(i * 128, (i + 1) * 128)
            pending_casts.append((wdb[rs_, :], w_down[rs_, :], Rwd))

        def emit_casts(n):
            for _ in range(min(n, len(pending_casts))):
                o_, i_, r_ = pending_casts.pop(0)
                P.dma("pool", o_, i_, w=[r_])

        xt = [P.sb(f"xt{i}", [128, D], F32) for i in range(2)]
        xn = [P.sb(f"xn{i}", [128, D], BF16) for i in range(2)]
        junk = P.sb("junk", [128, D], BF16)
        Rxt = P.regs(2, "xt")
        Rxn = P.regs(2, "xn")
        Rjunk = P.reg("junk")
        st1 = P.sb("st1", [128, 16], F32)
        Rst1 = P.regs(16, "st1")
        o_all = P.sb("o_all", [128, NT, D], BF16)
        Roall = P.regs(NT, "oall")
        cpt = P.sb("cpt", [128, NT, 8], F32)
        Rcp = P.regs(NT, "cp")
        mgb = P.sb("mgb", [128, 4, 8], F32)
        Rmgb = P.reg("mgb")
        biasT = P.sb("biasT", [128, 4, NT, 8], F32)
        Rbias = P.reg("bias")
        caug = P.sb("caug", [128, NT, 8], F32)
        caug_h32 = P.sb("caug_h32", [128, NT, 8], F32)
        caug_hl = P.sb("caug_hl", [128, NT, 8, 2], BF16)
        Rcaug = P.reg("caug")
        zt = P.sb("zt", [128, 2, 8], F32)
        Rzt = P.regs(2, "zt")
        pT = [P.sb(f"pT{i}", [128, 512], BF16) for i in range(4)]
        RpT = P.regs(4, "pT")
        oTf = [P.sb(f"oTf{i}", [128, 512], BF16) for i in range(2)]
        RoTf = P.regs(2, "oTf")
        sq = [P.sb(f"sq{i}", [128, 256], F32) for i in range(4)]
        Rsq = P.regs(4, "sq")
        ssr = [P.sb(f"ssr{i}", [128, 4], F32) for i in range(4)]
        Rssr = P.regs(4, "ssr")
        qn = [P.sb(f"qn{i}", [128, 4, HD], F32) for i in range(4)]
        Rqn = P.regs(4, "qn")
        rtmp = [P.sb(f"rtmp{i}", [128, 4, 4, 8], F32) for i in range(4)]
        Rrtmp = P.regs(4, "rtmp")
        qr = [P.sb(f"qr{i}", [128, 4, HD], BF16) for i in range(4)]
        Rqr = P.regs(4, "qr")
        qaug = [P.sb(f"qaug{i}", [128, 2, 66], BF16) for i in range(4)]
        kaug = [P.sb(f"kaug{i}", [128, 2, 66], BF16) for i in range(4)]
        Rqaug = P.regs(4, "qaug")
        Rkaug = P.regs(4, "kaug")
        sa_t = [P.sb(f"sa{i}", [128, 512], F32) for i in range(2)]
        Rsa = P.regs(2, "sa")
        rden = [P.sb(f"rden{i}", [128, 4], F32) for i in range(2)]
        Rrden = P.regs(2, "rden")
        for i in range(4):
            P.op("dve", lambda e, i=i: e.memset(kaug[i][:], 1.0), w=[Rkaug[i]])

        psum = P.ps("psum", [128, 8, 512], F32)
        Rps = P.regs(8, "bank")
        for g_ in Rps:
            g_.excl = True

        ARENA_B = 92160 - 16384
        arena = P.sb("arena", [128, ARENA_B // 2], BF16)
        hT = P.sb("hT", [128, KC, S], BF16)
        wf_sb = P.sb("wf_sb", [128, KC, 8], BF16)

        def carve(off, shape, dt=BF16):
            n = 1
            for s_ in shape:
                n *= s_
            if dt == BF16:
                v = arena[:, off // 2: off // 2 + n]
                nb = 2 * n
            else:
                v = arena[:, off // 2: off // 2 + 2 * n].bitcast(F32)
                nb = 4 * n
            return _v(v, shape), off + nb

        off = 0
        wj = []
        for i in range(2):
            t_, off = carve(off, [KC, 384])
            wj.append(t_)
        vd = []
        qkTd, off = carve(off, [2, S])
        for i in range(3):
            t_, off = carve(off, [NT, 2, 65])
            vd.append(t_)
        qTf, off = carve(off, [2, S])
        kTf, off = carve(off, [2, S])
        vf, off = carve(off, [NT, 2, 65])
        assert off <= ARENA_B, off
        off = 0
        x1, off = carve(off, [4, D], F32)
        h2T, off = carve(off, [KC, 512])
        gT, off = carve(off, [NFC, 512])
        wgu = []
        wo_sb, _ = carve(off, [KC, D])
        for i in range(2):
            t_, off = carve(off, [2, KC, 256])
            wgu.append(t_)
        wd_sb = []
        wgu_alt, _ = carve(off, [2, KC, 256])
        for i in range(2):
            t_, off = carve(off, [2, D])
            wd_sb.append(t_)
        onT = []
        for i in range(2):
            t_, off = carve(off, [KC, 128])
            onT.append(t_)
        assert off <= ARENA_B, off

        RhT = P.regs(NT, "hT")
        Rwj = P.regs(2, "wj")
        RqkTd = P.regs(NT, "qkTd")
        Rvd = [P.regs(NT, "vd0_"), P.regs(4, "vd1_"), [P.reg("vd2")]]
        RqTf = P.regs(NT, "qTf")
        RkTf = P.regs(NT, "kTf")
        Rvf = P.regs(NT, "vf")
        A_regs = Rwj + RqkTd + sum(Rvd, []) + RqTf + RkTf + Rvf
        Rwo_sb = P.reg("wo_sb")
        Rx1 = P.regs(4, "x1")
        Rh2T = P.regs(4, "h2T")
        RgT = P.regs(NFC, "gT")
        Rwgu2 = [P.regs(2, "wguA"), P.regs(2, "wguB")]
        Rwgu = Rwgu2[0] + Rwgu2[1]
        Rwd_sb = P.regs(2, "wd_sb")
        RonT = P.regs(2, "onT")
        BC_regs = Rx1 + Rh2T + RgT + Rwgu + Rwd_sb + RonT
        Rvscr = P.regs(2, "vscr")
        Rout = P.reg("out")
        Rxin = P.reg("xin")

        def bank(i, shape=None, parts=128):
            v = psum[0:parts, i, :]
            return v

        tsl = lambda T: slice(T * 128, (T + 1) * 128)

        def rstd_ops(col, n):
            c = st1[:, col:col + 1]
            P.act(c, c, AF.Ln, r=[Rst1[col]], w=[Rst1[col]], scale=1.0 / n, bias=EPS)
            P.act(c, c, AF.Exp, r=[Rst1[col]], w=[Rst1[col]], scale=-0.5)


        Rwf_sb = P.reg("wf_sb")
        P.dma("sp", wf_sb[:], wfb.rearrange("(c p) n -> p c n", p=128), r=[Rwf], w=[Rwf_sb])

        def head_stages(s, T):
            k = T % 2

            def a1():
                P.dma(HEAD_DMA_Q, xt[k][:], x[s, tsl(T), :], r=[Rxin], w=[Rxt[k]])

            def a2():
                P.act(junk[:], xt[k][:], AF.Square, r=[Rxt[k]], w=[Rjunk, Rst1[k]], accum_out=st1[:, k:k + 1])
                rstd_ops(k, D)

            def a3():
                P.stt("dve", xn[k][:], xt[k][:], st1[:, k:k + 1], gmix_b[:], ALU.mult, ALU.mult,
                      r=[Rxt[k], Rst1[k], Rc], w=[Rxn[k]])

            def a4():
                for half in range(2):
                    b = 4 + half
                    bv = _v(bank(b), [4, 128])
                    for c in range(4):
                        kc = half * 4 + c
                        P.mm(bv[:, c, :], xn[k][:, kc * 128:(kc + 1) * 128], ident[:], r=[Rxn[k], Rc], w=[Rps[b]])
                    P.cp("act" if half == 0 else "dve", hT[:, half * 4:(half + 1) * 4, tsl(T)], bv,
                         r=[Rps[b]], w=[RhT[T]])

            def g1():
                pf = bank(6)[:, 0:8]
                for kc in range(KC):
                    P.mm(pf, hT[:, kc, tsl(T)], wf_sb[:, kc, :], start=(kc == 0), stop=(kc == KC - 1),
                         r=[RhT[T], Rwf_sb], w=[Rps[6]])
                P.tt("dve", zt[:, k, :], pf, bf_b[:], ALU.add, r=[Rps[6], Rc], w=[Rzt[k]])

            def g2():
                P.act(zt[:, k, :], zt[:, k, :], AF.Exp, r=[Rzt[k]], w=[Rzt[k]], scale=-1.0)
                P.act(zt[:, k, :], zt[:, k, :], AF.Ln, r=[Rzt[k]], w=[Rzt[k]], bias=1.0, scale=1.0)

            def g3():
                pc = bank(7)[:, 0:8]
                P.mm(pc, tri[:], zt[:, k, :], start=True, stop=(T == 0), r=[Rc, Rzt[k]], w=[Rps[7]])
                if T > 0:
                    P.mm(pc, sel[:], cpt[:, T - 1, :], start=False, stop=True, r=[Rc, Rcp[T - 1]], w=[Rps[7]])
                P.cp("dve", cpt[:, T, :], pc, r=[Rps[7]], w=[Rcp[T]])
            return [a1, a2, a3, a4, g1, g2, g3]

        def gate_fin(s):
            P.op("dve", lambda e: e.memset(mgb[:, 0, :], 0.0), w=[Rmgb])
            for G in range(1, 4):
                pm = bank(6)[:, 8 * G:8 * G + 8]
                P.mm(pm, sel[:], cpt[:, 4 * G - 1, :], r=[Rc, Rcp[4 * G - 1]], w=[Rps[6]])
                P.cp("dve", mgb[:, G, :], pm, r=[Rps[6]], w=[Rmgb])
            for G in range(4):
                nj = 4 * G + 4
                P.tt("dve", biasT[:, G, 0:nj, :], cpt[:, 0:nj, :], mgb[:, G, :].unsqueeze(1).broadcast_to([128, nj, 8]),
                     ALU.subtract, r=Rcp + [Rmgb], w=[Rbias])
                P.tt("dve", caug[:, 4 * G:4 * G + 4, :], mgb[:, G, :].unsqueeze(1).broadcast_to([128, 4, 8]),
                     cpt[:, 4 * G:4 * G + 4, :], ALU.subtract, r=Rcp + [Rmgb], w=[Rcaug])
            P.cp("dve", caug_hl[:, :, :, 0], caug[:], r=[Rcaug], w=[Rcaug])
            P.cp("dve", caug_h32[:], caug_hl[:, :, :, 0], r=[Rcaug], w=[Rcaug])
            P.tt("dve", caug_h32[:], caug[:], caug_h32[:], ALU.subtract, r=[Rcaug], w=[Rcaug])
            P.cp("dve", caug_hl[:, :, :, 1], caug_h32[:], r=[Rcaug], w=[Rcaug])

        def head_batches(s):
            batches = []
            for b_ in range(4):
                tiles = list(range(5 * b_, min(NT, 5 * b_ + 5)))
                slots = {}
                for li, T in enumerate(tiles):
                    for si, f in enumerate(head_stages(s, T)):
                        slots.setdefault(li + si, []).append((si, f))
                lst = []
                for sl in sorted(slots):
                    fs = [f for _, f in sorted(slots[sl], key=lambda t: -t[0])]
                    lst.append(lambda fs=fs: [f() for f in fs])
                if b_ == 3:
                    lst.append(lambda: gate_fin(s))
                batches.append(lst)
            return batches

        try:
          chk("consts")
          for s in range(nseq):
              if s > 0:
                  P.handoff(BC_regs, A_regs)
              for i in range(3):
                  P.op("pool", lambda e, i=i: e.memset(vd[i][:, :, :, 64:65], 1.0), w=Rvd[i])
              P.op("pool", lambda e: e.memset(vf[:, :, :, 64:65], 1.0), w=Rvf)
              if FOX_K > 66:
                  P.op("pool", lambda e: e.memset(qTf[64:128, :, :], 0.0), w=RqTf)
                  P.op("pool", lambda e: e.memset(kTf[64:128, :, :], 0.0), w=RkTf)
              if s == 0:
                  for bl_ in head_batches(0):
                      for f_ in bl_:
                          f_()
              chk("A0")
              chk("gate")

              def prep_begin(j):
                  ws = j % 2
                  P.dma("sp", wj[ws][:], winb[:, j, :].rearrange("(c p) n -> p c n", p=128), r=[Rwin[j]], w=[Rwj[ws]])


              def prep_stages(j, T):
                  pair = j // 2
                  is_dil = (j % 2 == 0)
                  ws = j % 2
                  k = T % 4
                  pb = T % 2
                  pj = bank(pb)[:, 0:384]
                  rs_b = ssr[k][:].unsqueeze(2).broadcast_to([128, 4, HD])
                  ptr = bank(2)

                  def s1a():
                      for kc in range(KC // 2):
                          P.mm(pj, hT[:, kc, tsl(T)], wj[ws][:, kc, :], start=(kc == 0), stop=False,
                               r=[RhT[T], Rwj[ws]], w=[Rps[pb]])

                  def s1():
                      if PREP_SPLIT:
                          rng = range(KC // 2, KC)
                      else:
                          rng = range(KC)
                      for kc in rng:
                          P.mm(pj, hT[:, kc, tsl(T)], wj[ws][:, kc, :], start=(kc == 0), stop=(kc == KC - 1),
                               r=[RhT[T], Rwj[ws]], w=[Rps[pb]])
                      P.act(sq[k][:], pj[:, 0:256], AF.Square, r=[Rps[pb]], w=[Rsq[k]])
                      if is_dil:
                          P.cp(VCOPY_ENG, vd[0][:, T, :, 0:64], _v(pj[:, 256:384], [2, HD]), r=[Rps[pb]], w=[Rvd[0][T]])
                      else:
                          P.cp(VCOPY_ENG, vf[:, T, :, 0:64], _v(pj[:, 256:384], [2, HD]), r=[Rps[pb]], w=[Rvf[T]])

                  def s2():
                      P.op("dve", lambda e: e.tensor_reduce(out=ssr[k][:], in_=_v(sq[k][:], [4, HD]), axis=AX.X, op=ALU.add),
                           r=[Rsq[k]], w=[Rssr[k]])

                  def s3():
                      P.act(ssr[k][:], ssr[k][:], AF.Ln, r=[Rssr[k]], w=[Rssr[k]], bias=HD * EPS, scale=1.0)
                      P.act(ssr[k][:], ssr[k][:], AF.Exp, r=[Rssr[k]], w=[Rssr[k]], scale=-0.5)

                  def s4():
                      if is_dil:
                          P.tt("dve", qn[k][:], _v(pj[:, 0:256], [4, HD]), rs_b, ALU.mult, r=[Rps[pb], Rssr[k]], w=[Rqn[k]])
                          P.tt("dve", qn[k][:], qn[k][:], gd_b[:], ALU.mult, r=[Rqn[k], Rc], w=[Rqn[k]])
                          P.cp("dve", qr[k][:, :, 16:64], qn[k][:, :, 16:64], r=[Rqn[k]], w=[Rqr[k]])
                      else:
                          P.tt("dve", qaug[k][:, :, 0:64], _v(pj[:, 0:128], [2, HD]), rs_b[:, 0:2, :], ALU.mult,
                               r=[Rps[pb], Rssr[k]], w=[Rqaug[k]])
                          P.tt("dve", kaug[k][:, :, 0:64], _v(pj[:, 128:256], [2, HD]), rs_b[:, 2:4, :], ALU.mult,
                               r=[Rps[pb], Rssr[k]], w=[Rkaug[k]])
                          P.cp("pool", qaug[k][:, :, 64:66], caug_hl[:, T, 2 * pair:2 * pair + 2, :], r=[Rcaug], w=[Rqaug[k]])

                  def s5():
                      if is_dil:
                          cb = cos_t[:, T, :].unsqueeze(1).broadcast_to([128, 4, 8])
                          sb_ = sin_t[:, T, :].unsqueeze(1).broadcast_to([128, 4, 8])
                          x1_ = qn[k][:, :, 0:8]
                          x2_ = qn[k][:, :, 8:16]
                          rt = rtmp[k]
                          P.tt(ROPE_ENG, rt[:, 0], x1_, cb, ALU.mult, r=[Rqn[k], Rc], w=[Rrtmp[k]])
                          P.tt(ROPE_ENG, rt[:, 1], x2_, sb_, ALU.mult, r=[Rqn[k], Rc], w=[Rrtmp[k]])
                          P.tt(ROPE_ENG, rt[:, 2], x2_, cb, ALU.mult, r=[Rqn[k], Rc], w=[Rrtmp[k]])
                          P.tt(ROPE_ENG, rt[:, 3], x1_, sb_, ALU.mult, r=[Rqn[k], Rc], w=[Rrtmp[k]])
                          P.tt(ROPE_ENG, qr[k][:, :, 0:8], rt[:, 0], rt[:, 1], ALU.subtract, r=[Rrtmp[k]], w=[Rqr[k]])
                          P.tt(ROPE_ENG, qr[k][:, :, 8:16], rt[:, 2], rt[:, 3], ALU.add, r=[Rrtmp[k]], w=[Rqr[k]])

                  def s6():
                      if is_dil:
                          ptv = _v(ptr[:, 0:256], [2, 128])
                          for i in range(2):
                              P.mm(ptv[:, i, :], qr[k][:, 2 * i:2 * i + 2, :].rearrange("p a b -> p (a b)"), ident[:],
                                   r=[Rqr[k], Rc], w=[Rps[2]])
                      else:
                          ptv = _v(ptr[0:66, :], [4, 128])
                          for i in range(2):
                              P.mm(ptv[:, i, :], qaug[k][:, i, :], ident[:], r=[Rqaug[k], Rc], w=[Rps[2]])
                          for i in range(2):
                              P.mm(ptv[:, 2 + i, :], kaug[k][:, i, :], ident[:], r=[Rkaug[k], Rc], w=[Rps[2]])

                  def s7():
                      if is_dil:
                          ptv = _v(ptr[:, 0:256], [2, 128])
                          P.cp("dve", qkTd[:, :, tsl(T)], ptv, r=[Rps[2]], w=[RqkTd[T]])
                      else:
                          ptv = _v(ptr[0:66, :], [4, 128])
                          P.op("dve", lambda e: e.tensor_scalar_mul(out=qTf[0:66, :, tsl(T)], in0=ptv[:, 0:2, :], scalar1=gs_f[0:66, 0:1]),
                               r=[Rps[2], Rc], w=[RqTf[T]])
                          P.cp("dve", kTf[0:66, :, tsl(T)], ptv[:, 2:4, :], r=[Rps[2]], w=[RkTf[T]])
                  def s67():
                      s6()
                      s7()
                  return ([s1a] if PREP_SPLIT else []) + [s1, s2, s3, s4, s5, s67]

              def prep_list(j, stride=PREP_STRIDE):
                  slots = {}
                  for T in range(NT):
                      for si, f in enumerate(prep_stages(j, T)):
                          slots.setdefault(T * stride + si, []).append((si, f))
                  lst = [lambda: prep_begin(j)]
                  for sl in sorted(slots):
                      fs = [f for _, f in sorted(slots[sl], key=lambda t: -t[0])]
                      lst.append(lambda fs=fs: [f() for f in fs])
                  lst.append(lambda: prep_end(j))
                  return lst

              def prep_end(j):
                  if j % 2 == 0:
                      vs = (j // 2) % 2
                      P.dma(BOUNCE_Q, vscr[vs].rearrange("(t p) c -> p t c", p=128), vd[0][:].rearrange("p t a b -> p t (a b)"),
                            r=Rvd[0], w=[Rvscr[vs]])
                      for m_ in range(4):
                          P.dma(BOUNCE_Q, vd[1][:].rearrange("p (r m) a b -> p m r (a b)", r=4)[:, m_],
                                vscr[vs].rearrange("(m p r) c -> p m r c", m=4, r=4)[:, m_], r=[Rvscr[vs]], w=[Rvd[1][m_]])
                      P.dma(BOUNCE_Q, vd[2][:].rearrange("p r a b -> p r (a b)"),
                            vscr[vs].rearrange("(p r) c -> p r c", r=16), r=[Rvscr[vs]], w=Rvd[2])

              pipe_ctr = {"s": 0, "p": 0, "o": 0}

              def s_bank():
                  b_ = 3 + pipe_ctr["s"] % 3
                  pipe_ctr["s"] += 1
                  return b_

              def p_slot():
                  i_ = pipe_ctr["p"] % 4
                  pipe_ctr["p"] += 1
                  return i_

              def o_bank():
                  b_ = 6 + pipe_ctr["o"] % 2
                  pipe_ctr["o"] += 1
                  return b_

              def attn_steps(j):
                  pair = j // 2
                  is_dil = (j % 2 == 0)
                  steps = []
                  if not is_dil:
                      for hh in range(2):
                          h = 2 * pair + hh
                          for G in range(4):
                              nJ = 4 * G + 4
                              grp = {}
                              for J in range(nJ):
                                  def fnA(J=J, G=G, hh=hh, h=h, grp=grp):
                                      if J == 0:
                                          grp["ob"] = o_bank()
                                      o0 = max(0, (J - 4 * G) * 128)
                                      sb_i = s_bank()
                                      sv = bank(sb_i)
                                      P.mm(sv[:, o0:512], kTf[0:FOX_K, hh, tsl(J)], qTf[0:FOX_K, hh, G * 512 + o0:(G + 1) * 512],
                                           r=[RkTf[J]] + RqTf[4 * G:4 * G + 4], w=[Rps[sb_i]])
                                      pi = p_slot()
                                      grp[J] = pi
                                      P.act(pT[pi][:, o0:512], sv[:, o0:512], AF.Exp, r=[Rps[sb_i], Rbias], w=[RpT[pi]],
                                            bias=biasT[:, G, J, h:h + 1], scale=1.0)
                                      if J >= 4 * G:
                                          P.op("pool", lambda e, pi=pi, o0=o0: e.affine_select(
                                              out=pT[pi][:, o0:o0 + 128], in_=pT[pi][:, o0:o0 + 128], pattern=[[1, 128]],
                                              compare_op=ALU.is_ge, fill=0.0, base=0, channel_multiplier=-1),
                                              r=[RpT[pi]], w=[RpT[pi]])

                                  def fnB(J=J, G=G, hh=hh, h=h, grp=grp, nJ=nJ):
                                      o0 = max(0, (J - 4 * G) * 128)
                                      ob = grp["ob"]
                                      ov = bank(ob)[0:65, :]
                                      pi = grp[J]
                                      P.mm(ov[:, o0:512], vf[:, J, hh, :], pT[pi][:, o0:512], start=(J == 0), stop=(J == nJ - 1),
                                           r=[Rvf[J], RpT[pi]], w=[Rps[ob]])
                                      if J != nJ - 1:
                                          return None
                                      ei = G % 2
                                      P.cp("dve", oTf[ei][0:65, :], ov, r=[Rps[ob]], w=[RoTf[ei]])

                                      def fin():
                                          tv = _v(bank(2)[:, 0:260], [4, 65])
                                          for t in range(4):
                                              P.mm(tv[:, t, :], oTf[ei][0:65, t * 128:(t + 1) * 128], ident[0:65, 0:65],
                                                   r=[RoTf[ei], Rc], w=[Rps[2]])
                                          P.op("dve", lambda e: e.reciprocal(out=rden[ei][:], in_=tv[:, :, 64]),
                                               r=[Rps[2]], w=[Rrden[ei]])
                                          P.tt("dve", o_all[:, 4 * G:4 * G + 4, h * 64:(h + 1) * 64], tv[:, :, 0:64],
                                               rden[ei][:].unsqueeze(2).broadcast_to([128, 4, 64]), ALU.mult,
                                               r=[Rps[2], Rrden[ei]], w=Roall[4 * G:4 * G + 4])
                                      return [fin]
                                  steps.append((fnA, fnB))
                  else:
                      for hh in range(2):
                          h = 2 * pair + hh
                          rows = slice(64 * hh, 64 * hh + 64)
                          for G in range(4):
                              grp = {"n": 0}

                              def mkstep(sblocks, width, mrows, mask, pvs, grp=grp, G=G, h=h):
                                  loc = {}
                                  idx = grp["n"]
                                  grp["n"] += 1

                                  def fnA():
                                      if idx == 0:
                                          grp["ob"] = o_bank()
                                      sb_i = s_bank()
                                      sv = bank(sb_i)
                                      for lt, rh, c0, n, M in sblocks:
                                          P.mm(sv[0:M, c0:c0 + n], lt, rh, r=RqkTd, w=[Rps[sb_i]])
                                      pi = p_slot()
                                      loc["pi"] = pi
                                      P.act(pT[pi][0:mrows, 0:width], sv[0:mrows, 0:width], AF.Exp, r=[Rps[sb_i]], w=[RpT[pi]])
                                      P.tt(DIL_MASK_ENG(pi), pT[pi][0:mrows, 0:width], pT[pi][0:mrows, 0:width], mask, ALU.mult,
                                           r=[RpT[pi], Rc], w=[RpT[pi]])

                                  def fnB():
                                      ob = grp["ob"]
                                      ov = bank(ob)[0:65, :]
                                      pi = loc["pi"]
                                      last = (idx == grp["n"] - 1)
                                      if idx == 0:
                                          P.mm(ov, zeros_t[0:1, 0:65], zeros_t[0:1, 0:512], start=True, stop=False,
                                               r=[Rc], w=[Rps[ob]])
                                      for ii, (vt, K, c0, n, osl, Rv) in enumerate(pvs):
                                          P.mm(ov[:, osl], vt, pT[pi][0:K, c0:c0 + n], start=False,
                                               stop=(last and ii == len(pvs) - 1), r=Rv + [RpT[pi]], w=[Rps[ob]])
                                      if not last:
                                          return None
                                      ei = G % 2
                                      P.cp("dve", oTf[ei][0:65, :], ov, r=[Rps[ob]], w=[RoTf[ei]])

                                      def fin():
                                          tv = _v(bank(2)[:, 0:260], [4, 65])
                                          for t in range(4):
                                              P.mm(tv[:, t, :], oTf[ei][0:65, t * 128:(t + 1) * 128], ident[0:65, 0:65],
                                                   r=[RoTf[ei], Rc], w=[Rps[2]])
                                          P.op("dve", lambda e: e.reciprocal(out=rden[ei][:], in_=tv[:, :, 64]),
                                               r=[Rps[2]], w=[Rrden[ei]])
                                          P.tt("dve", o_all[:, 4 * G:4 * G + 4, 512 + h * 64:512 + (h + 1) * 64], tv[:, :, 0:64],
                                               rden[ei][:].unsqueeze(2).broadcast_to([128, 4, 64]), ALU.mult,
                                               r=[Rps[2], Rrden[ei]], w=Roall[4 * G:4 * G + 4])
                                      return [fin]
                                  steps.append((fnA, fnB))

                              def d1_blk(J):
                                  if J == 4 * G - 1:
                                      return (J, 4 * G, 128)
                                  if J == 4 * G + 3:
                                      return (J, J, 128)
                                  return (J, J, 256)
                              if G >= 1:
                                  packs = [([4 * G - 1, 4 * G, 4 * G + 3], mBC2[:, 0:512]), ([4 * G + 1, 4 * G + 2], mCB2[:, 0:512])]
                              else:
                                  packs = [([0, 3], mCB2[:, 0:384]), ([1, 2], mCB2[:, 0:512])]
                              for Jl, mask in packs:
                                  sbl, pvs = [], []
                                  c0 = 0
                                  for J in Jl:
                                      J, qb, n = d1_blk(J)
                                      oc = (qb - 4 * G) * 128
                                      sbl.append((qkTd[rows, 1, tsl(J)], qkTd[rows, 0, qb * 128:qb * 128 + n], c0, n, 128))
                                      pvs.append((vd[0][:, J, hh, :], 128, c0, n, slice(oc, oc + n), [Rvd[0][J]]))
                                      c0 += n
                                  mkstep(sbl, c0, 128, mask, pvs)
                              ms = ([G - 1, G] if G >= 1 else [G])
                              rpacks = [[0, 1], [2, 3]] if G >= 1 else [[0, 1, 2, 3]]
                              for rl in rpacks:
                                  sbl, pvs = [], []
                                  c0 = 0
                                  for r4 in rl:
                                      qs = slice(512 * G + r4, 512 * (G + 1), 4)
                                      for m in ms:
                                          ks = slice(512 * m + r4, 512 * (m + 1), 4)
                                          sbl.append((qkTd[rows, 1, ks], qkTd[rows, 0, qs], c0, 128, 128))
                                          pvs.append((vd[1][:, r4 * 4 + m, hh, :], 128, c0, 128, slice(r4, 512, 4), [Rvd[1][m]]))
                                          c0 += 128
                                  mask = mBC2[:, 0:512] if G >= 1 else mC4[:, 0:512]
                                  mkstep(sbl, c0, 128, mask, pvs)
                              M = 32 * (G + 1)
                              sbl, pvs = [], []
                              for r16 in range(16):
                                  ks = slice(r16, min(r16 + 16 * M, S), 16)
                                  qs = slice(512 * G + r16, 512 * (G + 1), 16)
                                  sbl.append((qkTd[rows, 1, ks], qkTd[rows, 0, qs], 32 * r16, 32, M))
                                  pvs.append((vd[2][0:M, r16, hh, :], M, 32 * r16, 32, slice(r16, 512, 16), Rvd[2]))
                              mkstep(sbl, 512, M, m16[0:M, G, :], pvs)
                  return steps

              def run_pipeline(steps, side, look=PIPE_LOOK, defer=PIPE_DEFER):
                  n = len(steps)
                  pend = []
                  nside = len(side)
                  sidx = 0
                  if SIDE_FIRST:
                      for f_ in side:
                          f_()
                      sidx = nside
                  emitted = 0
                  for i in range(n):
                      while emitted < min(n, i + look + 1):
                          steps[emitted][0]()
                          emitted += 1
                      d = steps[i][1]()
                      pend = [(c - 1, f) for c, f in pend]
                      while pend and pend[0][0] <= 0:
                          pend.pop(0)[1]()
                      if d:
                          for f in d:
                              pend.append((defer, f))
                      want = min(nside, ((i + 1) * nside) // max(1, int(SIDE_FRAC * n)))
                      while sidx < want:
                          side[sidx]()
                          sidx += 1
                  for _, f in pend:
                      f()
                  while sidx < nside:
                      side[sidx]()
                      sidx += 1

              for f_ in prep_list(0):
                  f_()
              chk("prep0")
              for j in range(8):
                  side = []
                  if j + 1 < 8:
                      side += prep_list(j + 1)
                  if s == 0:
                      side.append(lambda: emit_casts(6))
                  run_pipeline(attn_steps(j), side)
                  chk(f"attn{j}")

              if s == 0:
                  dump("d_oall", o_all[:], Roall)
              chk("attn")
              emit_casts(1000)
              P.handoff(A_regs, BC_regs)
              wctr = 0
              dctr = 0
              side_b = head_batches(s + 1) if s + 1 < nseq else [[], [], [], []]
              P.dma("sp", wo_sb, woutb.rearrange("(c p) n -> p c n", p=128), r=[Rwo], w=Rwgu)
              for g in range(4):
                  P.dma("sp", wgu_alt[:, 0], wgb[:, 0:256].rearrange("(c p) n -> p c n", p=128), r=[Rwg], w=[Rwd_sb[0]])
                  P.dma("sp", wgu_alt[:, 1], wub[:, 0:256].rearrange("(c p) n -> p c n", p=128), r=[Rwu], w=[Rwd_sb[1]])
                  def pb_stages(t, g=g, s=s):
                      T = 4 * g + t
                      k = t % 2
                      c0 = 4 + 3 * t

                      def b1():
                          P.act(junk[:, 0:512], o_all[:, T, 0:512], AF.Square, r=[Roall[T]], w=[Rjunk, Rst1[c0]],
                                accum_out=st1[:, c0:c0 + 1])
                          P.act(junk[:, 512:1024], o_all[:, T, 512:1024], AF.Square, r=[Roall[T]], w=[Rjunk, Rst1[c0 + 1]],
                                accum_out=st1[:, c0 + 1:c0 + 2])

                      def b2():
                          rstd_ops(c0, 512)
                          rstd_ops(c0 + 1, 512)
                          P.dma(HEAD_DMA_Q, x1[:, t, :], x[s, tsl(T), :], r=[Rxin], w=[Rx1[t]])

                      def b3():
                          P.stt("dve", xn[k][:, 0:512], o_all[:, T, 0:512], st1[:, c0:c0 + 1], gout_b[:, 0:512],
                                ALU.mult, ALU.mult, r=[Roall[T], Rst1[c0], Rc], w=[Rxn[k]])
                          P.stt("dve", xn[k][:, 512:1024], o_all[:, T, 512:1024], st1[:, c0 + 1:c0 + 2], gout_b[:, 512:1024],
                                ALU.mult, ALU.mult, r=[Roall[T], Rst1[c0 + 1], Rc], w=[Rxn[k]])

                      def b4():
                          for half in range(2):
                              b = 2 * k + half
                              bv = _v(bank(b), [4, 128])
                              for c in range(4):
                                  kc = half * 4 + c
                                  P.mm(bv[:, c, :], xn[k][:, kc * 128:(kc + 1) * 128], ident[:], r=[Rxn[k], Rc], w=[Rps[b]])
                              P.cp("act" if half == 0 else "dve", onT[k][:, half * 4:(half + 1) * 4, :], bv, r=[Rps[b]], w=[RonT[k]])

                      def b5():
                          for half in range(2):
                              b = 4 + half
                              for kc in range(KC):
                                  P.mm(bank(b), onT[k][:, kc, :], wo_sb[:, kc, half * 512:(half + 1) * 512],
                                       start=(kc == 0), stop=(kc == KC - 1), r=[RonT[k]] + Rwgu, w=[Rps[b]])
                              P.tt("dve", x1[:, t, half * 512:(half + 1) * 512], x1[:, t, half * 512:(half + 1) * 512], bank(b), ALU.add,
                                   r=[Rx1[t], Rps[b]], w=[Rx1[t]])

                      def b6():
                          P.act(junk[:], x1[:, t, :], AF.Square, r=[Rx1[t]], w=[Rjunk, Rst1[c0 + 2]], accum_out=st1[:, c0 + 2:c0 + 3])
                          rstd_ops(c0 + 2, D)

                      def b7():
                          P.stt("dve", xn[k][:], x1[:, t, :], st1[:, c0 + 2:c0 + 3], gffn_b[:], ALU.mult, ALU.mult,
                                r=[Rx1[t], Rst1[c0 + 2], Rc], w=[Rxn[k]])

                      def b8():
                          for half in range(2):
                              b = 2 * k + half
                              bv = _v(bank(b), [4, 128])
                              for c in range(4):
                                  kc = half * 4 + c
                                  P.mm(bv[:, c, :], xn[k][:, kc * 128:(kc + 1) * 128], ident[:], r=[Rxn[k], Rc], w=[Rps[b]])
                              P.cp("act" if half == 0 else "dve", h2T[:, half * 4:(half + 1) * 4, t * 128:(t + 1) * 128], bv,
                                   r=[Rps[b]], w=[Rh2T[t]])
                      return [b1, b2, b3, b4, b5, b6, b7, b8]

                  pslots = {}
                  for t in range(4):
                      for si, f in enumerate(pb_stages(t)):
                          pslots.setdefault(t + si, []).append((si, f))
                  for sl in sorted(pslots):
                      for _, f in sorted(pslots[sl], key=lambda q_: -q_[0]):
                          f()
                  chk(f"B{g}")
                  for fp in range(NFC // 2):
                      csl = slice(fp * 256, (fp + 1) * 256)
                      if fp == 0:
                          wcur, Rwcur = wgu_alt, [Rwd_sb[0], Rwd_sb[1]]
                      else:
                          wsl = wctr % 2
                          wctr += 1
                          wcur, Rwcur = wgu[wsl], Rwgu2[wsl]
                          P.dma("sp", wcur[:, 0], wgb[:, csl].rearrange("(c p) n -> p c n", p=128), r=[Rwg], w=[Rwcur[0]])
                          P.dma("sp", wcur[:, 1], wub[:, csl].rearrange("(c p) n -> p c n", p=128), r=[Rwu], w=[Rwcur[1]])
                      for f2 in range(2):
                          fc = 2 * fp + f2
                          ba = 0 + 2 * (fc % 2)
                          bu = 1 + 2 * (fc % 2)
                          for which, bb in ((0, ba), (1, bu)):
                              for kc in range(KC):
                                  P.mm(bank(bb), wcur[:, which, kc, f2 * 128:(f2 + 1) * 128], h2T[:, kc, :],
                                       start=(kc == 0), stop=(kc == KC - 1), r=[Rwcur[which]] + Rh2T, w=[Rps[bb]])
                          si = fc % 2
                          sa = sa_t[si][:]
                          P.act(sa, bank(ba), AF.Silu, r=[Rps[ba]], w=[Rsa[si]])
                          P.tt("dve", gT[:, fc, :], sa, bank(bu), ALU.mult, r=[Rsa[si], Rps[bu]], w=[RgT[fc]])
                      if side_b[g]:
                          side_b[g].pop(0)()
                  chk(f"gu{g}")
                  for fp in range(NFC // 2):
                      i_ = dctr % 2
                      dctr += 1
                      P.dma("sp", wd_sb[i_][:], wdb[fp * 256:(fp + 1) * 256, :].rearrange("(a p) n -> p a n", p=128),
                            r=[Rwd], w=[Rwd_sb[i_]])
                      if fp == 1 and g < 3:
                          P.dma("sp", wo_sb, woutb.rearrange("(c p) n -> p c n", p=128), r=[Rwo], w=Rwgu)
                      for f2 in range(2):
                          fc = 2 * fp + f2
                          for t in range(4):
                              for half in range(2):
                                  b = 2 * t + half
                                  P.mm(bank(b), gT[:, fc, t * 128:(t + 1) * 128], wd_sb[i_][:, f2, half * 512:(half + 1) * 512],
                                       start=(fc == 0), stop=(fc == NFC - 1), r=[RgT[fc], Rwd_sb[i_]], w=[Rps[b]])
                  for t in range(4):
                      T = 4 * g + t
                      for half in range(2):
                          b = 2 * t + half
                          P.tt("dve", x1[:, t, half * 512:(half + 1) * 512], x1[:, t, half * 512:(half + 1) * 512], bank(b), ALU.add,
                               r=[Rx1[t], Rps[b]], w=[Rx1[t]])
                      P.dma("pool", out[s, tsl(T), :], x1[:, t, :], r=[Rx1[t]], w=[P.reg()])
              assert not any(side_b), "head work left over"
        except _Stop:
            pass
        P.finish()
        build.stats = P.stats
    return nc


def host_consts():
    bf = ml_dtypes.bfloat16
    ident = np.eye(128, dtype=np.float32).astype(bf)
    kk = np.arange(128)[:, None]
    mm_ = np.arange(128)[None, :]
    tri = (kk <= mm_).astype(np.float32)
    sel = np.zeros((128, 128), np.float32)
    sel[127, :] = 1.0
    half = 8
    inv_freq = np.power(np.float32(ROPE_THETA), -np.arange(half, dtype=np.float32) * np.float32(2.0) / np.float32(16)).astype(np.float32)
    pos = (np.arange(NT)[None, :] * 128 + np.arange(128)[:, None]).astype(np.float32)
    ang = (pos[:, :, None] * inv_freq[None, None, :]).astype(np.float32)
    cos = np.cos(ang.astype(np.float64)).astype(np.float32).reshape(128, NT * 8)
    sin = np.sin(ang.astype(np.float64)).astype(np.float32).reshape(128, NT * 8)
    causal = (mm_ >= kk).astype(np.float32)
    band = (mm_ <= kk).astype(np.float32)
    mcb = np.concatenate([causal, band], axis=1).astype(bf)
    mc4 = np.concatenate([causal] * 4, axis=1).astype(bf)
    nn = np.arange(128)
    perm4 = np.zeros((128, 128), np.float32)
    perm4[4 * (nn % 32) + nn // 32, nn] = 1.0
    perm16 = np.zeros((128, 128), np.float32)
    perm16[16 * (nn % 8) + nn // 8, nn] = 1.0
    mbc = np.concatenate([band, causal], axis=1).astype(bf)
    pp = np.arange(128)[:, None, None, None]
    gg = np.arange(4)[None, :, None, None]
    ii = np.arange(32)[None, None, None, :]
    m16 = np.broadcast_to((pp <= 32 * gg + ii), (128, 4, 16, 32)).astype(np.float32).reshape(128, 4 * 512).astype(bf)
    return {"c_mbc": mbc, "c_m16": m16, "c_perm4": perm4.astype(bf), "c_perm16": perm16.astype(bf), "c_ident": ident, "c_tri": tri, "c_sel": sel, "c_cos": cos, "c_sin": sin, "c_mcb": mcb, "c_mc4": mc4}


_PARAMS = ("g_mix", "w_in", "b_forget", "g_q_fox", "g_k_fox", "g_q_dil", "g_k_dil", "g_out_fox", "g_out_dil",
           "w_out", "g_ffn", "w_gate", "w_up", "w_down")


def make_in_maps(inputs, nseq, ncores):
    consts = host_consts()
    shared = dict(consts)
    for k in _PARAMS:
        a = np.ascontiguousarray(np.asarray(inputs[k], dtype=np.float32))
        a = a.reshape(a.shape[1:]) if a.ndim == 3 else a.reshape(1, -1)
        shared[k] = a
    x = np.asarray(inputs["x"], dtype=np.float32)
    maps = []
    for c in range(ncores):
        m = dict(shared)
        m["x"] = np.ascontiguousarray(x[c * nseq:(c + 1) * nseq])
        maps.append(m)
    return maps


def kernel(**inputs):
    nseq = 4
    nc = build(nseq)
    in_maps = make_in_maps(inputs, nseq, NCORES)
    res = run_bass_kernel_spmd(nc, in_maps, core_ids=list(range(NCORES)))
    return np.concatenate([np.asarray(r["out"]) for r in res.results], axis=0).astype(np.float32)
```

```python
import contextlib
import math
import numpy as np
import ml_dtypes
import concourse.bass as bass
import concourse.mybir as mybir
from concourse.bass_utils import run_bass_kernel_spmd

F32 = mybir.dt.float32
BF16 = mybir.dt.bfloat16
ALU = mybir.AluOpType
AF = mybir.ActivationFunctionType
AX = mybir.AxisListType

D = 1024
KC = 8
S = 2048
NT = 16
HD = 64
DFF = 2816
NFC = 22
INC = 3080
EPS = 1e-6
NCORES = 8
ROPE_THETA = 500000.0
ROPE_ENG = "pool"
D4_ENG = "act"
PIPE_LOOK = 2
HEAD_DMA_Q = "act"
SIDE_FRAC = 0.8
FOX_K = 66
BOUNCE_Q = "sp"
PREP_STRIDE = 2
PREP_SPLIT = False
VCOPY_ENG = "dve"
DIL_MASK_ENG = lambda pi: "dve"
SIDE_FIRST = False
PIPE_DEFER = 3


class _Stop(Exception):
    pass


class Reg:
    __slots__ = ("name", "lw", "rd", "excl")

    def __init__(self, name):
        self.name = name
        self.lw = None
        self.rd = []
        self.excl = False


class Op:
    __slots__ = ("eng", "fn", "deps", "dma", "sem", "val", "signal", "cnt", "waits", "idx")

    def __init__(self, eng, fn, deps, dma, idx):
        self.eng = eng
        self.fn = fn
        self.deps = deps
        self.dma = dma
        self.sem = None
        self.val = 0
        self.signal = False
        self.cnt = 0
        self.waits = []
        self.idx = idx


ENGS = ("pe", "act", "dve", "pool", "sp")
BLOCKNAME = {"pe": "tensor", "act": "scalar", "dve": "vector", "pool": "gpsimd", "sp": "sync"}


class Prog:
    def __init__(self, nc, es, dma_pool=8):
        self.nc = nc
        self.es = es
        self.ops = []
        self.dma_pool = dma_pool
        self.nreg = 0

    def sb(self, name, shape, dt):
        return self.es.enter_context(self.nc.sbuf_tensor(name, list(shape), dt))

    def ps(self, name, shape, dt):
        return self.es.enter_context(self.nc.psum_tensor(name, list(shape), dt))

    def reg(self, name=None):
        self.nreg += 1
        return Reg(name or f"r{self.nreg}")

    def regs(self, n, name="r"):
        return [self.reg(f"{name}{i}") for i in range(n)]

    max_ops = None

    def op(self, eng, fn, r=(), w=(), dma=False):
        i = len(self.ops)
        if Prog.max_ops is not None and i >= Prog.max_ops:
            raise _Stop()
        if any(g.excl for g in r):
            w = list(w) + [g for g in r if g.excl]
            r = [g for g in r if not g.excl]
        deps = {}
        for g in r:
            if g.lw is not None:
                deps[g.lw] = True
        for g in w:
            if g.lw is not None:
                deps.setdefault(g.lw, False)
            for j in g.rd:
                deps.setdefault(j, False)
        for g in r:
            g.rd.append(i)
        for g in w:
            g.lw = i
            g.rd = []
        self.ops.append(Op(eng, fn, deps, dma, i))
        return i

    def handoff(self, old, new):
        allops = set()
        for g in old:
            if g.lw is not None:
                allops.add(g.lw)
            allops.update(g.rd)
        lst = sorted(allops)
        for g in new:
            g.rd = list(set(g.rd) | set(lst))

    def dma(self, q, out, in_, r=(), w=()):
        return self.op(q, lambda e: e.dma_start(out=out, in_=in_), r=r, w=w, dma=True)

    def mm(self, out, lhsT, rhs, start=True, stop=True, r=(), w=()):
        return self.op("pe", lambda e: e.matmul(out, lhsT=lhsT, rhs=rhs, start=start, stop=stop), r=r, w=w)

    def act(self, out, in_, func, r=(), w=(), **kw):
        return self.op("act", lambda e: e.activation(out=out, in_=in_, func=func, **kw), r=r, w=w)

    def tt(self, eng, out, in0, in1, op, r=(), w=()):
        return self.op(eng, lambda e: e.tensor_tensor(out=out, in0=in0, in1=in1, op=op), r=r, w=w)

    def stt(self, eng, out, in0, scalar, in1, op0, op1, r=(), w=()):
        return self.op(eng, lambda e: e.scalar_tensor_tensor(out=out, in0=in0, scalar=scalar, in1=in1, op0=op0, op1=op1),
                       r=r, w=w)

    def cp(self, eng, out, in_, r=(), w=()):
        if eng == "act":
            return self.op("act", lambda e: e.activation(out=out, in_=in_, func=AF.Copy), r=r, w=w)
        return self.op(eng, lambda e: e.tensor_copy(out=out, in_=in_), r=r, w=w)

    def finish(self):
        nc = self.nc
        ops = self.ops
        LIM = 16000
        sem_eng_l = {e: [] for e in ENGS}

        def sem_for(e, cnt):
            ep = (cnt - 1) // LIM
            while len(sem_eng_l[e]) <= ep:
                sem_eng_l[e].append(self.es.enter_context(nc.semaphore(f"s_{e}{len(sem_eng_l[e])}")))
            return sem_eng_l[e][ep], (cnt - 1) % LIM + 1
        dma_sems = {e: [self.es.enter_context(nc.semaphore(f"d_{e}{k}")) for k in range(self.dma_pool)]
                    for e in ("sp", "act", "pool")}
        dcount = {e: 0 for e in dma_sems}
        prev_on_sem = {}
        for o in ops:
            if o.dma:
                k = dcount[o.eng]
                dcount[o.eng] += 1
                slot = k % self.dma_pool
                o.sem = dma_sems[o.eng][slot]
                o.val = 16 * (k // self.dma_pool + 1)
                key = (o.eng, slot)
                if key in prev_on_sem:
                    o.deps.setdefault(prev_on_sem[key], False)
                prev_on_sem[key] = o.idx
        seen_eng = {e: {p: -1 for p in ENGS} for e in ENGS}
        seen_dma = {e: {} for e in ENGS}
        need = []
        for o in ops:
            e = o.eng
            best = {}
            dwaits = {}
            for d, is_raw in o.deps.items():
                Dp = ops[d]
                if Dp.dma:
                    key = id(Dp.sem)
                    if seen_dma[e].get(key, 0) >= Dp.val:
                        continue
                    if key not in dwaits or dwaits[key][1] < Dp.val:
                        dwaits[key] = (Dp.sem, Dp.val)
                else:
                    p = Dp.eng
                    if p == e and (e == "pe" or not is_raw):
                        continue
                    if d <= seen_eng[e][p]:
                        continue
                    if p not in best or best[p] < d:
                        best[p] = d
            o.waits = []
            for key, (sem, val) in dwaits.items():
                seen_dma[e][key] = val
                o.waits.append((sem, val))
            prods = []
            for p, d in best.items():
                seen_eng[e][p] = d
                ops[d].signal = True
                prods.append(d)
            need.append(prods)
        run = {e: 0 for e in ENGS}
        for o in ops:
            if o.dma:
                continue
            if o.signal:
                run[o.eng] += 1
            o.cnt = run[o.eng]
        for o, prods in zip(ops, need):
            for d in prods:
                Dp = ops[d]
                o.waits.append(sem_for(Dp.eng, Dp.cnt))
        final_waits = []
        for key, idx in prev_on_sem.items():
            Dp = ops[idx]
            final_waits.append((Dp.sem, Dp.val))
        by_eng = {e: [o for o in ops if o.eng == e] for e in ENGS}
        self.stats = {e: len(by_eng[e]) for e in ENGS}
        self.stats["signals"] = sum(1 for o in ops if o.signal)
        self.stats["waits"] = sum(len(o.waits) for o in ops)
        with nc.Block() as block:
            for e in ENGS:
                lst = by_eng[e]

                def f(eng, lst=lst, e=e):
                    for o in lst:
                        for sem, val in o.waits:
                            eng.wait_ge(sem, val)
                        ins = o.fn(eng)
                        if o.dma:
                            ins.then_inc(o.sem, 16)
                        elif o.signal:
                            ins.then_inc(sem_for(e, o.cnt)[0], 1)
                    if e == "sp":
                        for sem, val in final_waits:
                            eng.wait_ge(sem, val)

                getattr(block, BLOCKNAME[e])(f)


def _v(ap, shape):
    if len(shape) == 1:
        return ap
    if len(shape) == 2:
        return ap.rearrange("p (a b) -> p a b", a=shape[0])
    if len(shape) == 3:
        return ap.rearrange("p (a b c) -> p a b c", a=shape[0], b=shape[1])
    raise ValueError


def build(nseq=4, dbg=None, stop_after=None):
    nc = bass.Bass("TRN2", target_bir_lowering=False)
    build.dumps = []

    def chk(stage):
        if stop_after == stage:
            raise _Stop()

    def din(name, shape, dt=F32):
        return nc.dram_tensor(name, list(shape), dt, kind="ExternalInput").ap()

    def dscr(name, shape, dt=BF16):
        return nc.dram_tensor(name, list(shape), dt, kind="Internal").ap()

    x = din("x", [nseq, S, D])
    g_mix = din("g_mix", [1, D])
    w_in = din("w_in", [D, INC])
    b_forget = din("b_forget", [1, 8])
    g_q_fox = din("g_q_fox", [1, HD])
    g_k_fox = din("g_k_fox", [1, HD])
    g_q_dil = din("g_q_dil", [1, HD])
    g_k_dil = din("g_k_dil", [1, HD])
    g_out_fox = din("g_out_fox", [1, 512])
    g_out_dil = din("g_out_dil", [1, 512])
    w_out = din("w_out", [D, D])
    g_ffn = din("g_ffn", [1, D])
    w_gate = din("w_gate", [D, DFF])
    w_up = din("w_up", [D, DFF])
    w_down = din("w_down", [DFF, D])
    c_ident = din("c_ident", [128, 128], BF16)
    c_perm4 = din("c_perm4", [128, 128], BF16)
    c_perm16 = din("c_perm16", [128, 128], BF16)
    c_tri = din("c_tri", [128, 128])
    c_sel = din("c_sel", [128, 128])
    c_cos = din("c_cos", [128, NT * 8])
    c_sin = din("c_sin", [128, NT * 8])
    c_mcb = din("c_mcb", [128, 256], BF16)
    c_mc4 = din("c_mc4", [128, 512], BF16)
    c_mbc = din("c_mbc", [128, 256], BF16)
    c_m16 = din("c_m16", [128, 4 * 512], BF16)
    out = nc.dram_tensor("out", [nseq, S, D], F32, kind="ExternalOutput").ap()
    dbg_out = {}

    winb = dscr("winb", [D, 8, 384])
    wfb = dscr("wfb", [D, 8])
    woutb = dscr("woutb", [D, D])
    wgb = dscr("wgb", [D, DFF])
    wub = dscr("wub", [D, DFF])
    wdb = dscr("wdb", [DFF, D])
    vscr = [dscr(f"vscr{i}", [S, 130]) for i in range(2)]

    with contextlib.ExitStack() as es:
        P = Prog(nc, es)
        Rdump = P.reg("dump")

        def dump(name, ap, r):
            if not dbg or name not in dbg:
                return
            t_ = nc.dram_tensor(name, list(ap.shape), ap.dtype, kind="ExternalOutput").ap()
            build.dumps.append(name)
            P.dma("sp", t_, ap, r=r, w=[Rdump])
        ident = P.sb("ident", [128, 128], BF16)
        tri = P.sb("tri", [128, 128], F32)
        sel = P.sb("sel", [128, 128], F32)
        cos_t = P.sb("cos_t", [128, NT, 8], F32)
        sin_t = P.sb("sin_t", [128, NT, 8], F32)
        mcb = P.sb("mcb", [128, 256], BF16)
        mbc = P.sb("mbc", [128, 256], BF16)
        mC4 = P.sb("mC4", [128, 512], BF16)
        mCB2 = P.sb("mCB2", [128, 512], BF16)
        mBC2 = P.sb("mBC2", [128, 512], BF16)
        zeros_t = P.sb("zeros_t", [128, 512], BF16)
        m16 = P.sb("m16", [128, 4, 512], BF16)
        gmix_b = P.sb("gmix_b", [128, D], F32)
        gffn_b = P.sb("gffn_b", [128, D], F32)
        gout_b = P.sb("gout_b", [128, D], F32)
        gd_b = P.sb("gd_b", [128, 4, HD], F32)
        gfc = P.sb("gfc", [128, 2], F32)
        gs_f = P.sb("gs_f", [128, 1], F32)
        bf_b = P.sb("bf_b", [128, 8], F32)
        Rc = P.reg("consts")
        Rcl = []

        def cnew():
            Rcl.append(P.reg())
            return Rcl[-1]
        P.dma("sp", ident[:], c_ident, w=[cnew()])
        P.dma("sp", tri[:], c_tri, w=[cnew()])
        P.dma("sp", sel[:], c_sel, w=[cnew()])
        P.dma("sp", cos_t[:].rearrange("p a b -> p (a b)"), c_cos, w=[cnew()])
        P.dma("sp", sin_t[:].rearrange("p a b -> p (a b)"), c_sin, w=[cnew()])
        P.dma("sp", mcb[:], c_mcb, w=[cnew()])
        P.dma("sp", mbc[:], c_mbc, w=[cnew()])
        for h_ in range(2):
            P.dma("sp", mC4[:, h_ * 128:(h_ + 1) * 128], c_mcb[:, 0:128], w=[cnew()])
            P.dma("sp", mC4[:, (h_ + 2) * 128:(h_ + 3) * 128], c_mcb[:, 0:128], w=[cnew()])
            P.dma("sp", mCB2[:, h_ * 256:(h_ + 1) * 256], c_mcb, w=[cnew()])
            P.dma("sp", mBC2[:, h_ * 256:(h_ + 1) * 256], c_mbc, w=[cnew()])
        P.dma("sp", m16[:].rearrange("p a b -> p (a b)"), c_m16, w=[cnew()])
        P.dma("sp", gmix_b[:], g_mix.partition_broadcast(128), w=[cnew()])
        P.dma("sp", gffn_b[:], g_ffn.partition_broadcast(128), w=[cnew()])
        P.dma("sp", gout_b[:, 0:512], g_out_fox.partition_broadcast(128), w=[cnew()])
        P.dma("sp", gout_b[:, 512:1024], g_out_dil.partition_broadcast(128), w=[cnew()])
        P.dma("sp", gd_b[:, 0, :], g_q_dil.partition_broadcast(128), w=[cnew()])
        P.dma("sp", gd_b[:, 2, :], g_k_dil.partition_broadcast(128), w=[cnew()])
        P.dma("sp", bf_b[:], b_forget.partition_broadcast(128), w=[cnew()])
        P.dma("sp", gfc[0:64, 0:1], g_q_fox.rearrange("o e -> e o"), w=[cnew()])
        P.dma("sp", gfc[0:64, 1:2], g_k_fox.rearrange("o e -> e o"), w=[cnew()])
        P.op("dve", lambda e: e.tensor_copy(out=gd_b[:, 1, :], in_=gd_b[:, 0, :]), r=Rcl, w=[Rc])
        P.op("dve", lambda e: e.tensor_scalar_mul(out=gd_b[:, 2, :], in0=gd_b[:, 2, :], scalar1=8.0),
             r=[Rc], w=[Rc])
        P.op("dve", lambda e: e.tensor_copy(out=gd_b[:, 3, :], in_=gd_b[:, 2, :]), r=[Rc], w=[Rc])
        P.op("dve", lambda e: e.memset(gs_f[:], 1.0), w=[Rc])
        P.op("dve", lambda e: e.memset(zeros_t[:], 0.0), w=[Rc])
        P.stt("dve", gs_f[0:64, :], gfc[0:64, 0:1], 8.0, gfc[0:64, 1:2], ALU.mult, ALU.mult, r=[Rc], w=[Rc])

        Rwin = [[] for _ in range(8)]
        Rwf = P.reg("wfb")
        Rwo, Rwg, Rwu, Rwd = [], [], [], []
        def job_cols(j):
            p = j // 2
            if j % 2 == 0:
                return [1544 + 128 * p, 2056 + 128 * p, 2568 + 128 * p]
            return [128 * p, 512 + 128 * p, 1024 + 128 * p]
        P.dma("pool", wfb, w_in[:, 1536:1544], w=[Rwf])
        for j in range(8):
            for part, c0 in enumerate(job_cols(j)):
                Rwin[j].append(P.reg())
                P.dma("pool", winb[:, j, part * 128:(part + 1) * 128], w_in[:, c0:c0 + 128], w=[Rwin[j][-1]])
        pending_casts = []
        for i in range(KC):
            rs_ = slice(i * 128, (i + 1) * 128)
            pending_casts.append((woutb[rs_, :], w_out[rs_, :], Rwo))
        for i in range(KC):
            rs_ = slice(i * 128, (i + 1) * 128)
            pending_casts.append((wgb[rs_, :], w_gate[rs_, :], Rwg))
            pending_casts.append((wub[rs_, :], w_up[rs_, :], Rwu))
        for i in range(NFC):
            rs_ = slice(i * 128, (i + 1) * 128)
            pending_casts.append((wdb[rs_, :], w_down[rs_, :], Rwd))

        def emit_casts(n):
            for _ in range(min(n, len(pending_casts))):
                o_, i_, r_ = pending_casts.pop(0)
                r_.append(P.reg())
                P.dma("pool", o_, i_, w=[r_[-1]])

        xt = [P.sb(f"xt{i}", [128, D], F32) for i in range(2)]
        xn = [P.sb(f"xn{i}", [128, D], BF16) for i in range(2)]
        junk = P.sb("junk", [128, D], BF16)
        Rxt = P.regs(2, "xt")
        Rxn = P.regs(2, "xn")
        Rjunk = P.reg("junk")
        st1 = P.sb("st1", [128, 16], F32)
        Rst1 = P.regs(16, "st1")
        o_all = P.sb("o_all", [128, NT, D], BF16)
        Roall = P.regs(NT, "oall")
        cpt = P.sb("cpt", [128, NT, 8], F32)
        Rcp = P.regs(NT, "cp")
        mgb = P.sb("mgb", [128, 4, 8], F32)
        Rmgb = P.reg("mgb")
        biasT = P.sb("biasT", [128, 4, NT, 8], F32)
        Rbias = P.reg("bias")
        caug = P.sb("caug", [128, NT, 8], F32)
        caug_h32 = P.sb("caug_h32", [128, NT, 8], F32)
        caug_hl = P.sb("caug_hl", [128, NT, 8, 2], BF16)
        Rcaug = P.reg("caug")
        zt = P.sb("zt", [128, 2, 8], F32)
        Rzt = P.regs(2, "zt")
        pT = [P.sb(f"pT{i}", [128, 512], BF16) for i in range(4)]
        RpT = P.regs(4, "pT")
        oTf = [P.sb(f"oTf{i}", [128, 512], BF16) for i in range(2)]
        RoTf = P.regs(2, "oTf")
        sq = [P.sb(f"sq{i}", [128, 256], F32) for i in range(4)]
        Rsq = P.regs(4, "sq")
        ssr = [P.sb(f"ssr{i}", [128, 4], F32) for i in range(4)]
        Rssr = P.regs(4, "ssr")
        qn = [P.sb(f"qn{i}", [128, 4, HD], F32) for i in range(4)]
        Rqn = P.regs(4, "qn")
        rtmp = [P.sb(f"rtmp{i}", [128, 4, 4, 8], F32) for i in range(4)]
        Rrtmp = P.regs(4, "rtmp")
        qr = [P.sb(f"qr{i}", [128, 4, HD], BF16) for i in range(4)]
        Rqr = P.regs(4, "qr")
        qaug = [P.sb(f"qaug{i}", [128, 2, 66], BF16) for i in range(4)]
        kaug = [P.sb(f"kaug{i}", [128, 2, 66], BF16) for i in range(4)]
        Rqaug = P.regs(4, "qaug")
        Rkaug = P.regs(4, "kaug")
        sa_t = [P.sb(f"sa{i}", [128, 512], F32) for i in range(2)]
        Rsa = P.regs(2, "sa")
        rden = [P.sb(f"rden{i}", [128, 4], F32) for i in range(2)]
        Rrden = P.regs(2, "rden")
        for i in range(4):
            P.op("dve", lambda e, i=i: e.memset(kaug[i][:], 1.0), w=[Rkaug[i]])

        psum = P.ps("psum", [128, 8, 512], F32)
        Rps = P.regs(8, "bank")
        for g_ in Rps:
            g_.excl = True

        ARENA_B = 92160 - 16384
        arena = P.sb("arena", [128, ARENA_B // 2], BF16)
        hT = P.sb("hT", [128, KC, S], BF16)
        wf_sb = P.sb("wf_sb", [128, KC, 8], BF16)

        def carve(off, shape, dt=BF16):
            n = 1
            for s_ in shape:
                n *= s_
            if dt == BF16:
                v = arena[:, off // 2: off // 2 + n]
                nb = 2 * n
            else:
                v = arena[:, off // 2: off // 2 + 2 * n].bitcast(F32)
                nb = 4 * n
            return _v(v, shape), off + nb

        off = 0
        wj = []
        for i in range(2):
            t_, off = carve(off, [KC, 384])
            wj.append(t_)
        vd = []
        qkTd, off = carve(off, [2, S])
        for i in range(3):
            t_, off = carve(off, [NT, 2, 65])
            vd.append(t_)
        qTf, off = carve(off, [2, S])
        kTf, off = carve(off, [2, S])
        vf, off = carve(off, [NT, 2, 65])
        assert off <= ARENA_B, off
        off = 0
        x1, off = carve(off, [4, D], F32)
        h2T, off = carve(off, [KC, 512])
        gT, off = carve(off, [NFC, 512])
        wgu = []
        wo_sb, _ = carve(off, [KC, D])
        for i in range(2):
            t_, off = carve(off, [2, KC, 256])
            wgu.append(t_)
        wd_sb = []
        wgu_alt, _ = carve(off, [2, KC, 256])
        for i in range(2):
            t_, off = carve(off, [2, D])
            wd_sb.append(t_)
        onT = []
        for i in range(2):
            t_, off = carve(off, [KC, 128])
            onT.append(t_)
        assert off <= ARENA_B, off

        RhT = P.regs(NT, "hT")
        Rwj = P.regs(2, "wj")
        RqkTd = P.regs(NT, "qkTd")
        Rvd = [P.regs(NT, "vd0_"), P.regs(4, "vd1_"), [P.reg("vd2")]]
        RqTf = P.regs(NT, "qTf")
        RkTf = P.regs(NT, "kTf")
        Rvf = P.regs(NT, "vf")
        A_regs = Rwj + RqkTd + sum(Rvd, []) + RqTf + RkTf + Rvf
        Rwo_sb = P.reg("wo_sb")
        Rx1 = P.regs(4, "x1")
        Rh2T = P.regs(4, "h2T")
        RgT = P.regs(NFC, "gT")
        Rwgu2 = [P.regs(2, "wguA"), P.regs(2, "wguB")]
        Rwgu = Rwgu2[0] + Rwgu2[1]
        Rwd_sb = P.regs(2, "wd_sb")
        RonT = P.regs(2, "onT")
        BC_regs = Rx1 + Rh2T + RgT + Rwgu + Rwd_sb + RonT
        Rvscr = P.regs(2, "vscr")
        Rout = P.reg("out")
        Rxin = P.reg("xin")

        def bank(i, shape=None, parts=128):
            v = psum[0:parts, i, :]
            return v

        tsl = lambda T: slice(T * 128, (T + 1) * 128)

        def rstd_ops(col, n):
            c = st1[:, col:col + 1]
            P.act(c, c, AF.Ln, r=[Rst1[col]], w=[Rst1[col]], scale=1.0 / n, bias=EPS)
            P.act(c, c, AF.Exp, r=[Rst1[col]], w=[Rst1[col]], scale=-0.5)


        Rwf_sb = P.reg("wf_sb")
        P.dma("sp", wf_sb[:], wfb.rearrange("(c p) n -> p c n", p=128), r=[Rwf], w=[Rwf_sb])

        def head_stages(s, T):
            k = T % 2

            def a1():
                P.dma(HEAD_DMA_Q, xt[k][:], x[s, tsl(T), :], r=[Rxin], w=[Rxt[k]])

            def a2():
                P.act(junk[:], xt[k][:], AF.Square, r=[Rxt[k]], w=[Rjunk, Rst1[k]], accum_out=st1[:, k:k + 1])
                rstd_ops(k, D)

            def a3():
                P.stt("dve", xn[k][:], xt[k][:], st1[:, k:k + 1], gmix_b[:], ALU.mult, ALU.mult,
                      r=[Rxt[k], Rst1[k], Rc], w=[Rxn[k]])

            def a4():
                for half in range(2):
                    b = 4 + half
                    bv = _v(bank(b), [4, 128])
                    for c in range(4):
                        kc = half * 4 + c
                        P.mm(bv[:, c, :], xn[k][:, kc * 128:(kc + 1) * 128], ident[:], r=[Rxn[k], Rc], w=[Rps[b]])
                    P.cp("act" if half == 0 else "dve", hT[:, half * 4:(half + 1) * 4, tsl(T)], bv,
                         r=[Rps[b]], w=[RhT[T]])

            def g1():
                pf = bank(6)[:, 0:8]
                for kc in range(KC):
                    P.mm(pf, hT[:, kc, tsl(T)], wf_sb[:, kc, :], start=(kc == 0), stop=(kc == KC - 1),
                         r=[RhT[T], Rwf_sb], w=[Rps[6]])
                P.tt("dve", zt[:, k, :], pf, bf_b[:], ALU.add, r=[Rps[6], Rc], w=[Rzt[k]])

            def g2():
                P.act(zt[:, k, :], zt[:, k, :], AF.Exp, r=[Rzt[k]], w=[Rzt[k]], scale=-1.0)
                P.act(zt[:, k, :], zt[:, k, :], AF.Ln, r=[Rzt[k]], w=[Rzt[k]], bias=1.0, scale=1.0)

            def g3():
                pc = bank(7)[:, 0:8]
                P.mm(pc, tri[:], zt[:, k, :], start=True, stop=(T == 0), r=[Rc, Rzt[k]], w=[Rps[7]])
                if T > 0:
                    P.mm(pc, sel[:], cpt[:, T - 1, :], start=False, stop=True, r=[Rc, Rcp[T - 1]], w=[Rps[7]])
                P.cp("dve", cpt[:, T, :], pc, r=[Rps[7]], w=[Rcp[T]])
            return [a1, a2, a3, a4, g1, g2, g3]

        def gate_fin(s):
            P.op("dve", lambda e: e.memset(mgb[:, 0, :], 0.0), w=[Rmgb])
            for G in range(1, 4):
                pm = bank(6)[:, 8 * G:8 * G + 8]
                P.mm(pm, sel[:], cpt[:, 4 * G - 1, :], r=[Rc, Rcp[4 * G - 1]], w=[Rps[6]])
                P.cp("dve", mgb[:, G, :], pm, r=[Rps[6]], w=[Rmgb])
            for G in range(4):
                nj = 4 * G + 4
                P.tt("dve", biasT[:, G, 0:nj, :], cpt[:, 0:nj, :], mgb[:, G, :].unsqueeze(1).broadcast_to([128, nj, 8]),
                     ALU.subtract, r=Rcp + [Rmgb], w=[Rbias])
                P.tt("dve", caug[:, 4 * G:4 * G + 4, :], mgb[:, G, :].unsqueeze(1).broadcast_to([128, 4, 8]),
                     cpt[:, 4 * G:4 * G + 4, :], ALU.subtract, r=Rcp + [Rmgb], w=[Rcaug])
            P.cp("dve", caug_hl[:, :, :, 0], caug[:], r=[Rcaug], w=[Rcaug])
            P.cp("dve", caug_h32[:], caug_hl[:, :, :, 0], r=[Rcaug], w=[Rcaug])
            P.tt("dve", caug_h32[:], caug[:], caug_h32[:], ALU.subtract, r=[Rcaug], w=[Rcaug])
            P.cp("dve", caug_hl[:, :, :, 1], caug_h32[:], r=[Rcaug], w=[Rcaug])

        def head_batches(s):
            batches = []
            for b_ in range(4):
                tiles = list(range(5 * b_, min(NT, 5 * b_ + 5)))
                slots = {}
                for li, T in enumerate(tiles):
                    for si, f in enumerate(head_stages(s, T)):
                        slots.setdefault(li + si, []).append((si, f))
                lst = []
                for sl in sorted(slots):
                    fs = [f for _, f in sorted(slots[sl], key=lambda t: -t[0])]
                    lst.append(lambda fs=fs: [f() for f in fs])
                if b_ == 3:
                    lst.append(lambda: gate_fin(s))
                batches.append(lst)
            return batches

        try:
          chk("consts")
          for s in range(nseq):
              if s > 0:
                  P.handoff(BC_regs, A_regs)
              for i in range(3):
                  P.op("pool", lambda e, i=i: e.memset(vd[i][:, :, :, 64:65], 1.0), w=Rvd[i])
              P.op("pool", lambda e: e.memset(vf[:, :, :, 64:65], 1.0), w=Rvf)
              if FOX_K > 66:
                  P.op("pool", lambda e: e.memset(qTf[64:128, :, :], 0.0), w=RqTf)
                  P.op("pool", lambda e: e.memset(kTf[64:128, :, :], 0.0), w=RkTf)
              if s == 0:
                  for bl_ in head_batches(0):
                      for f_ in bl_:
                          f_()
              chk("A0")
              chk("gate")

              def prep_begin(j):
                  ws = j % 2
                  P.dma("sp", wj[ws][:], winb[:, j, :].rearrange("(c p) n -> p c n", p=128), r=Rwin[j], w=[Rwj[ws]])


              def prep_stages(j, T, pbanks=(0, 1)):
                  pair = j // 2
                  is_dil = (j % 2 == 0)
                  ws = j % 2
                  k = T % 4
                  pb = pbanks[T % len(pbanks)]
                  pj = bank(pb)[:, 0:384]
                  rs_b = ssr[k][:].unsqueeze(2).broadcast_to([128, 4, HD])
                  ptr = bank(2)

                  def s1a():
                      for kc in range(KC // 2):
                          P.mm(pj, hT[:, kc, tsl(T)], wj[ws][:, kc, :], start=(kc == 0), stop=False,
                               r=[RhT[T], Rwj[ws]], w=[Rps[pb]])

                  def s1():
                      if PREP_SPLIT:
                          rng = range(KC // 2, KC)
                      else:
                          rng = range(KC)
                      for kc in rng:
                          P.mm(pj, hT[:, kc, tsl(T)], wj[ws][:, kc, :], start=(kc == 0), stop=(kc == KC - 1),
                               r=[RhT[T], Rwj[ws]], w=[Rps[pb]])
                      P.act(sq[k][:], pj[:, 0:256], AF.Square, r=[Rps[pb]], w=[Rsq[k]])
                      if is_dil:
                          P.cp(VCOPY_ENG, vd[0][:, T, :, 0:64], _v(pj[:, 256:384], [2, HD]), r=[Rps[pb]], w=[Rvd[0][T]])
                      else:
                          P.cp(VCOPY_ENG, vf[:, T, :, 0:64], _v(pj[:, 256:384], [2, HD]), r=[Rps[pb]], w=[Rvf[T]])

                  def s2():
                      P.op("dve", lambda e: e.tensor_reduce(out=ssr[k][:], in_=_v(sq[k][:], [4, HD]), axis=AX.X, op=ALU.add),
                           r=[Rsq[k]], w=[Rssr[k]])

                  def s3():
                      P.act(ssr[k][:], ssr[k][:], AF.Ln, r=[Rssr[k]], w=[Rssr[k]], bias=HD * EPS, scale=1.0)
                      P.act(ssr[k][:], ssr[k][:], AF.Exp, r=[Rssr[k]], w=[Rssr[k]], scale=-0.5)

                  def s4():
                      if is_dil:
                          P.tt("dve", qn[k][:], _v(pj[:, 0:256], [4, HD]), rs_b, ALU.mult, r=[Rps[pb], Rssr[k]], w=[Rqn[k]])
                          P.tt("dve", qn[k][:], qn[k][:], gd_b[:], ALU.mult, r=[Rqn[k], Rc], w=[Rqn[k]])
                          P.cp("dve", qr[k][:, :, 16:64], qn[k][:, :, 16:64], r=[Rqn[k]], w=[Rqr[k]])
                      else:
                          P.tt("dve", qaug[k][:, :, 0:64], _v(pj[:, 0:128], [2, HD]), rs_b[:, 0:2, :], ALU.mult,
                               r=[Rps[pb], Rssr[k]], w=[Rqaug[k]])
                          P.tt("dve", kaug[k][:, :, 0:64], _v(pj[:, 128:256], [2, HD]), rs_b[:, 2:4, :], ALU.mult,
                               r=[Rps[pb], Rssr[k]], w=[Rkaug[k]])
                          P.cp("pool", qaug[k][:, :, 64:66], caug_hl[:, T, 2 * pair:2 * pair + 2, :], r=[Rcaug], w=[Rqaug[k]])

                  def s5():
                      if is_dil:
                          cb = cos_t[:, T, :].unsqueeze(1).broadcast_to([128, 4, 8])
                          sb_ = sin_t[:, T, :].unsqueeze(1).broadcast_to([128, 4, 8])
                          x1_ = qn[k][:, :, 0:8]
                          x2_ = qn[k][:, :, 8:16]
                          rt = rtmp[k]
                          P.tt(ROPE_ENG, rt[:, 0], x1_, cb, ALU.mult, r=[Rqn[k], Rc], w=[Rrtmp[k]])
                          P.tt(ROPE_ENG, rt[:, 1], x2_, sb_, ALU.mult, r=[Rqn[k], Rc], w=[Rrtmp[k]])
                          P.tt(ROPE_ENG, rt[:, 2], x2_, cb, ALU.mult, r=[Rqn[k], Rc], w=[Rrtmp[k]])
                          P.tt(ROPE_ENG, rt[:, 3], x1_, sb_, ALU.mult, r=[Rqn[k], Rc], w=[Rrtmp[k]])
                          P.tt(ROPE_ENG, qr[k][:, :, 0:8], rt[:, 0], rt[:, 1], ALU.subtract, r=[Rrtmp[k]], w=[Rqr[k]])
                          P.tt(ROPE_ENG, qr[k][:, :, 8:16], rt[:, 2], rt[:, 3], ALU.add, r=[Rrtmp[k]], w=[Rqr[k]])

                  def s6():
                      if is_dil:
                          ptv = _v(ptr[:, 0:256], [2, 128])
                          for i in range(2):
                              P.mm(ptv[:, i, :], qr[k][:, 2 * i:2 * i + 2, :].rearrange("p a b -> p (a b)"), ident[:],
                                   r=[Rqr[k], Rc], w=[Rps[2]])
                      else:
                          ptv = _v(ptr[0:66, :], [4, 128])
                          for i in range(2):
                              P.mm(ptv[:, i, :], qaug[k][:, i, :], ident[:], r=[Rqaug[k], Rc], w=[Rps[2]])
                          for i in range(2):
                              P.mm(ptv[:, 2 + i, :], kaug[k][:, i, :], ident[:], r=[Rkaug[k], Rc], w=[Rps[2]])

                  def s7():
                      if is_dil:
                          ptv = _v(ptr[:, 0:256], [2, 128])
                          P.cp("dve", qkTd[:, :, tsl(T)], ptv, r=[Rps[2]], w=[RqkTd[T]])
                      else:
                          ptv = _v(ptr[0:66, :], [4, 128])
                          P.op("dve", lambda e: e.tensor_scalar_mul(out=qTf[0:66, :, tsl(T)], in0=ptv[:, 0:2, :], scalar1=gs_f[0:66, 0:1]),
                               r=[Rps[2], Rc], w=[RqTf[T]])
                          P.cp("dve", kTf[0:66, :, tsl(T)], ptv[:, 2:4, :], r=[Rps[2]], w=[RkTf[T]])
                  def s67():
                      s6()
                      s7()
                  return ([s1a] if PREP_SPLIT else []) + [s1, s2, s3, s4, s5, s67]

              def prep_list(j, stride=PREP_STRIDE, pbanks=(0, 1)):
                  slots = {}
                  for T in range(NT):
                      for si, f in enumerate(prep_stages(j, T, pbanks)):
                          slots.setdefault(T * stride + si, []).append((si, f))
                  lst = [lambda: prep_begin(j)]
                  for sl in sorted(slots):
                      fs = [f for _, f in sorted(slots[sl], key=lambda t: -t[0])]
                      lst.append(lambda fs=fs: [f() for f in fs])
                  lst.append(lambda: prep_end(j))
                  return lst

              def prep_end(j):
                  if j % 2 == 0:
                      vs = (j // 2) % 2
                      P.dma(BOUNCE_Q, vscr[vs].rearrange("(t p) c -> p t c", p=128), vd[0][:].rearrange("p t a b -> p t (a b)"),
                            r=Rvd[0], w=[Rvscr[vs]])
                      for m_ in range(4):
                          P.dma(BOUNCE_Q, vd[1][:].rearrange("p (r m) a b -> p m r (a b)", r=4)[:, m_],
                                vscr[vs].rearrange("(m p r) c -> p m r c", m=4, r=4)[:, m_], r=[Rvscr[vs]], w=[Rvd[1][m_]])
                      P.dma(BOUNCE_Q, vd[2][:].rearrange("p r a b -> p r (a b)"),
                            vscr[vs].rearrange("(p r) c -> p r c", r=16), r=[Rvscr[vs]], w=Rvd[2])

              pipe_ctr = {"s": 0, "p": 0, "o": 0}

              def s_bank():
                  b_ = 3 + pipe_ctr["s"] % 3
                  pipe_ctr["s"] += 1
                  return b_

              def p_slot():
                  i_ = pipe_ctr["p"] % 4
                  pipe_ctr["p"] += 1
                  return i_

              def o_bank():
                  b_ = 6 + pipe_ctr["o"] % 2
                  pipe_ctr["o"] += 1
                  return b_

              def attn_steps(j):
                  pair = j // 2
                  is_dil = (j % 2 == 0)
                  steps = []
                  if not is_dil:
                      for hh in range(2):
                          h = 2 * pair + hh
                          for G in range(4):
                              nJ = 4 * G + 4
                              grp = {}
                              for J in range(nJ):
                                  def fnA(J=J, G=G, hh=hh, h=h, grp=grp):
                                      if J == 0:
                                          grp["ob"] = o_bank()
                                      o0 = max(0, (J - 4 * G) * 128)
                                      sb_i = s_bank()
                                      sv = bank(sb_i)
                                      P.mm(sv[:, o0:512], kTf[0:FOX_K, hh, tsl(J)], qTf[0:FOX_K, hh, G * 512 + o0:(G + 1) * 512],
                                           r=[RkTf[J]] + RqTf[4 * G:4 * G + 4], w=[Rps[sb_i]])
                                      pi = p_slot()
                                      grp[J] = pi
                                      P.act(pT[pi][:, o0:512], sv[:, o0:512], AF.Exp, r=[Rps[sb_i], Rbias], w=[RpT[pi]],
                                            bias=biasT[:, G, J, h:h + 1], scale=1.0)
                                      if J >= 4 * G:
                                          P.op("pool", lambda e, pi=pi, o0=o0: e.affine_select(
                                              out=pT[pi][:, o0:o0 + 128], in_=pT[pi][:, o0:o0 + 128], pattern=[[1, 128]],
                                              compare_op=ALU.is_ge, fill=0.0, base=0, channel_multiplier=-1),
                                              r=[RpT[pi]], w=[RpT[pi]])

                                  def fnB(J=J, G=G, hh=hh, h=h, grp=grp, nJ=nJ):
                                      o0 = max(0, (J - 4 * G) * 128)
                                      ob = grp["ob"]
                                      ov = bank(ob)[0:65, :]
                                      pi = grp[J]
                                      P.mm(ov[:, o0:512], vf[:, J, hh, :], pT[pi][:, o0:512], start=(J == 0), stop=(J == nJ - 1),
                                           r=[Rvf[J], RpT[pi]], w=[Rps[ob]])
                                      if J != nJ - 1:
                                          return None
                                      ei = G % 2
                                      P.cp("dve", oTf[ei][0:65, :], ov, r=[Rps[ob]], w=[RoTf[ei]])

                                      def fin():
                                          tv = _v(bank(2)[:, 0:260], [4, 65])
                                          for t in range(4):
                                              P.mm(tv[:, t, :], oTf[ei][0:65, t * 128:(t + 1) * 128], ident[0:65, 0:65],
                                                   r=[RoTf[ei], Rc], w=[Rps[2]])
                                          P.op("dve", lambda e: e.reciprocal(out=rden[ei][:], in_=tv[:, :, 64]),
                                               r=[Rps[2]], w=[Rrden[ei]])
                                          P.tt("dve", o_all[:, 4 * G:4 * G + 4, h * 64:(h + 1) * 64], tv[:, :, 0:64],
                                               rden[ei][:].unsqueeze(2).broadcast_to([128, 4, 64]), ALU.mult,
                                               r=[Rps[2], Rrden[ei]], w=Roall[4 * G:4 * G + 4])
                                      return [fin]
                                  steps.append((fnA, fnB))
                  else:
                      for hh in range(2):
                          h = 2 * pair + hh
                          rows = slice(64 * hh, 64 * hh + 64)
                          for G in range(4):
                              grp = {"n": 0}

                              def mkstep(sblocks, width, mrows, mask, pvs, grp=grp, G=G, h=h):
                                  loc = {}
                                  idx = grp["n"]
                                  grp["n"] += 1

                                  def fnA():
                                      if idx == 0:
                                          grp["ob"] = o_bank()
                                      sb_i = s_bank()
                                      sv = bank(sb_i)
                                      for lt, rh, c0, n, M in sblocks:
                                          P.mm(sv[0:M, c0:c0 + n], lt, rh, r=RqkTd, w=[Rps[sb_i]])
                                      pi = p_slot()
                                      loc["pi"] = pi
                                      P.act(pT[pi][0:mrows, 0:width], sv[0:mrows, 0:width], AF.Exp, r=[Rps[sb_i]], w=[RpT[pi]])
                                      P.tt(DIL_MASK_ENG(pi), pT[pi][0:mrows, 0:width], pT[pi][0:mrows, 0:width], mask, ALU.mult,
                                           r=[RpT[pi], Rc], w=[RpT[pi]])

                                  def fnB():
                                      ob = grp["ob"]
                                      ov = bank(ob)[0:65, :]
                                      pi = loc["pi"]
                                      last = (idx == grp["n"] - 1)
                                      if idx == 0:
                                          P.mm(ov, zeros_t[0:1, 0:65], zeros_t[0:1, 0:512], start=True, stop=False,
                                               r=[Rc], w=[Rps[ob]])
                                      for ii, (vt, K, c0, n, osl, Rv) in enumerate(pvs):
                                          P.mm(ov[:, osl], vt, pT[pi][0:K, c0:c0 + n], start=False,
                                               stop=(last and ii == len(pvs) - 1), r=Rv + [RpT[pi]], w=[Rps[ob]])
                                      if not last:
                                          return None
                                      ei = G % 2
                                      P.cp("dve", oTf[ei][0:65, :], ov, r=[Rps[ob]], w=[RoTf[ei]])

                                      def fin():
                                          tv = _v(bank(2)[:, 0:260], [4, 65])
                                          for t in range(4):
                                              P.mm(tv[:, t, :], oTf[ei][0:65, t * 128:(t + 1) * 128], ident[0:65, 0:65],
                                                   r=[RoTf[ei], Rc], w=[Rps[2]])
                                          P.op("dve", lambda e: e.reciprocal(out=rden[ei][:], in_=tv[:, :, 64]),
                                               r=[Rps[2]], w=[Rrden[ei]])
                                          P.tt("dve", o_all[:, 4 * G:4 * G + 4, 512 + h * 64:512 + (h + 1) * 64], tv[:, :, 0:64],
                                               rden[ei][:].unsqueeze(2).broadcast_to([128, 4, 64]), ALU.mult,
                                               r=[Rps[2], Rrden[ei]], w=Roall[4 * G:4 * G + 4])
                                      return [fin]
                                  steps.append((fnA, fnB))

                              def d1_blk(J):
                                  if J == 4 * G - 1:
                                      return (J, 4 * G, 128)
                                  if J == 4 * G + 3:
                                      return (J, J, 128)
                                  return (J, J, 256)
                              if G >= 1:
                                  packs = [([4 * G - 1, 4 * G, 4 * G + 3], mBC2[:, 0:512]), ([4 * G + 1, 4 * G + 2], mCB2[:, 0:512])]
                              else:
                                  packs = [([0, 3], mCB2[:, 0:384]), ([1, 2], mCB2[:, 0:512])]
                              for Jl, mask in packs:
                                  sbl, pvs = [], []
                                  c0 = 0
                                  for J in Jl:
                                      J, qb, n = d1_blk(J)
                                      oc = (qb - 4 * G) * 128
                                      sbl.append((qkTd[rows, 1, tsl(J)], qkTd[rows, 0, qb * 128:qb * 128 + n], c0, n, 128))
                                      pvs.append((vd[0][:, J, hh, :], 128, c0, n, slice(oc, oc + n), [Rvd[0][J]]))
                                      c0 += n
                                  mkstep(sbl, c0, 128, mask, pvs)
                              ms = ([G - 1, G] if G >= 1 else [G])
                              rpacks = [[0, 1], [2, 3]] if G >= 1 else [[0, 1, 2, 3]]
                              for rl in rpacks:
                                  sbl, pvs = [], []
                                  c0 = 0
                                  for r4 in rl:
                                      qs = slice(512 * G + r4, 512 * (G + 1), 4)
                                      for m in ms:
                                          ks = slice(512 * m + r4, 512 * (m + 1), 4)
                                          sbl.append((qkTd[rows, 1, ks], qkTd[rows, 0, qs], c0, 128, 128))
                                          pvs.append((vd[1][:, r4 * 4 + m, hh, :], 128, c0, 128, slice(r4, 512, 4), [Rvd[1][m]]))
                                          c0 += 128
                                  mask = mBC2[:, 0:512] if G >= 1 else mC4[:, 0:512]
                                  mkstep(sbl, c0, 128, mask, pvs)
                              M = 32 * (G + 1)
                              sbl, pvs = [], []
                              for r16 in range(16):
                                  ks = slice(r16, min(r16 + 16 * M, S), 16)
                                  qs = slice(512 * G + r16, 512 * (G + 1), 16)
                                  sbl.append((qkTd[rows, 1, ks], qkTd[rows, 0, qs], 32 * r16, 32, M))
                                  pvs.append((vd[2][0:M, r16, hh, :], M, 32 * r16, 32, slice(r16, 512, 16), Rvd[2]))
                              mkstep(sbl, 512, M, m16[0:M, G, :], pvs)
                  return steps

              def run_pipeline(steps, side, look=PIPE_LOOK, defer=PIPE_DEFER):
                  n = len(steps)
                  pend = []
                  nside = len(side)
                  sidx = 0
                  if SIDE_FIRST:
                      for f_ in side:
                          f_()
                      sidx = nside
                  emitted = 0
                  for i in range(n):
                      while emitted < min(n, i + look + 1):
                          steps[emitted][0]()
                          emitted += 1
                      d = steps[i][1]()
                      pend = [(c - 1, f) for c, f in pend]
                      while pend and pend[0][0] <= 0:
                          pend.pop(0)[1]()
                      if d:
                          for f in d:
                              pend.append((defer, f))
                      want = min(nside, ((i + 1) * nside) // max(1, int(SIDE_FRAC * n)))
                      while sidx < want:
                          side[sidx]()
                          sidx += 1
                  for _, f in pend:
                      f()
                  while sidx < nside:
                      side[sidx]()
                      sidx += 1

              for f_ in prep_list(0, stride=1, pbanks=(0, 1, 3, 4)):
                  f_()
              chk("prep0")
              for j in range(8):
                  side = []
                  if j + 1 < 8:
                      side += prep_list(j + 1)
                  if s == 0:
                      side.append(lambda: emit_casts(6))
                  run_pipeline(attn_steps(j), side)
                  chk(f"attn{j}")

              if s == 0:
                  dump("d_oall", o_all[:], Roall)
              chk("attn")
              emit_casts(1000)
              P.handoff(A_regs, BC_regs)
              wctr = 0
              dctr = 0
              side_b = head_batches(s + 1) if s + 1 < nseq else [[], [], [], []]
              P.dma("sp", wo_sb, woutb.rearrange("(c p) n -> p c n", p=128), r=Rwo, w=Rwgu)
              for g in range(4):
                  P.dma("sp", wgu_alt[:, 0], wgb[:, 0:256].rearrange("(c p) n -> p c n", p=128), r=Rwg, w=[Rwd_sb[0]])
                  P.dma("sp", wgu_alt[:, 1], wub[:, 0:256].rearrange("(c p) n -> p c n", p=128), r=Rwu, w=[Rwd_sb[1]])
                  def pb_stages(t, g=g, s=s):
                      T = 4 * g + t
                      k = t % 2
                      c0 = 4 + 3 * t

                      def b1():
                          P.act(junk[:, 0:512], o_all[:, T, 0:512], AF.Square, r=[Roall[T]], w=[Rjunk, Rst1[c0]],
                                accum_out=st1[:, c0:c0 + 1])
                          P.act(junk[:, 512:1024], o_all[:, T, 512:1024], AF.Square, r=[Roall[T]], w=[Rjunk, Rst1[c0 + 1]],
                                accum_out=st1[:, c0 + 1:c0 + 2])

                      def b2():
                          rstd_ops(c0, 512)
                          rstd_ops(c0 + 1, 512)
                          P.dma(HEAD_DMA_Q, x1[:, t, :], x[s, tsl(T), :], r=[Rxin], w=[Rx1[t]])

                      def b3():
                          P.stt("dve", xn[k][:, 0:512], o_all[:, T, 0:512], st1[:, c0:c0 + 1], gout_b[:, 0:512],
                                ALU.mult, ALU.mult, r=[Roall[T], Rst1[c0], Rc], w=[Rxn[k]])
                          P.stt("dve", xn[k][:, 512:1024], o_all[:, T, 512:1024], st1[:, c0 + 1:c0 + 2], gout_b[:, 512:1024],
                                ALU.mult, ALU.mult, r=[Roall[T], Rst1[c0 + 1], Rc], w=[Rxn[k]])

                      def b4():
                          for half in range(2):
                              b = 2 * k + half
                              bv = _v(bank(b), [4, 128])
                              for c in range(4):
                                  kc = half * 4 + c
                                  P.mm(bv[:, c, :], xn[k][:, kc * 128:(kc + 1) * 128], ident[:], r=[Rxn[k], Rc], w=[Rps[b]])
                              P.cp("act" if half == 0 else "dve", onT[k][:, half * 4:(half + 1) * 4, :], bv, r=[Rps[b]], w=[RonT[k]])

                      def b5():
                          for half in range(2):
                              b = 4 + half
                              for kc in range(KC):
                                  P.mm(bank(b), onT[k][:, kc, :], wo_sb[:, kc, half * 512:(half + 1) * 512],
                                       start=(kc == 0), stop=(kc == KC - 1), r=[RonT[k]] + Rwgu, w=[Rps[b]])
                              P.tt("dve", x1[:, t, half * 512:(half + 1) * 512], x1[:, t, half * 512:(half + 1) * 512], bank(b), ALU.add,
                                   r=[Rx1[t], Rps[b]], w=[Rx1[t]])

                      def b6():
                          P.act(junk[:], x1[:, t, :], AF.Square, r=[Rx1[t]], w=[Rjunk, Rst1[c0 + 2]], accum_out=st1[:, c0 + 2:c0 + 3])
                          rstd_ops(c0 + 2, D)

                      def b7():
                          P.stt("dve", xn[k][:], x1[:, t, :], st1[:, c0 + 2:c0 + 3], gffn_b[:], ALU.mult, ALU.mult,
                                r=[Rx1[t], Rst1[c0 + 2], Rc], w=[Rxn[k]])

                      def b8():
                          for half in range(2):
                              b = 2 * k + half
                              bv = _v(bank(b), [4, 128])
                              for c in range(4):
                                  kc = half * 4 + c
                                  P.mm(bv[:, c, :], xn[k][:, kc * 128:(kc + 1) * 128], ident[:], r=[Rxn[k], Rc], w=[Rps[b]])
                              P.cp("act" if half == 0 else "dve", h2T[:, half * 4:(half + 1) * 4, t * 128:(t + 1) * 128], bv,
                                   r=[Rps[b]], w=[Rh2T[t]])
                      return [b1, b2, b3, b4, b5, b6, b7, b8]

                  pslots = {}
                  for t in range(4):
                      for si, f in enumerate(pb_stages(t)):
                          pslots.setdefault(t + si, []).append((si, f))
                  for sl in sorted(pslots):
                      for _, f in sorted(pslots[sl], key=lambda q_: -q_[0]):
                          f()
                  chk(f"B{g}")
                  for fp in range(NFC // 2):
                      csl = slice(fp * 256, (fp + 1) * 256)
                      if fp == 0:
                          wcur, Rwcur = wgu_alt, [Rwd_sb[0], Rwd_sb[1]]
                      else:
                          wsl = wctr % 2
                          wctr += 1
                          wcur, Rwcur = wgu[wsl], Rwgu2[wsl]
                          P.dma("sp", wcur[:, 0], wgb[:, csl].rearrange("(c p) n -> p c n", p=128), r=Rwg, w=[Rwcur[0]])
                          P.dma("sp", wcur[:, 1], wub[:, csl].rearrange("(c p) n -> p c n", p=128), r=Rwu, w=[Rwcur[1]])
                      for f2 in range(2):
                          fc = 2 * fp + f2
                          ba = 0 + 2 * (fc % 2)
                          bu = 1 + 2 * (fc % 2)
                          for which, bb in ((0, ba), (1, bu)):
                              for kc in range(KC):
                                  P.mm(bank(bb), wcur[:, which, kc, f2 * 128:(f2 + 1) * 128], h2T[:, kc, :],
                                       start=(kc == 0), stop=(kc == KC - 1), r=[Rwcur[which]] + Rh2T, w=[Rps[bb]])
                          si = fc % 2
                          sa = sa_t[si][:]
                          P.act(sa, bank(ba), AF.Silu, r=[Rps[ba]], w=[Rsa[si]])
                          P.tt("dve", gT[:, fc, :], sa, bank(bu), ALU.mult, r=[Rsa[si], Rps[bu]], w=[RgT[fc]])
                      if side_b[g]:
                          side_b[g].pop(0)()
                  chk(f"gu{g}")
                  for fp in range(NFC // 2):
                      i_ = dctr % 2
                      dctr += 1
                      P.dma("sp", wd_sb[i_][:], wdb[fp * 256:(fp + 1) * 256, :].rearrange("(a p) n -> p a n", p=128),
                            r=Rwd, w=[Rwd_sb[i_]])
                      if fp == 1 and g < 3:
                          P.dma("sp", wo_sb, woutb.rearrange("(c p) n -> p c n", p=128), r=Rwo, w=Rwgu)
                      for f2 in range(2):
                          fc = 2 * fp + f2
                          for t in range(4):
                              for half in range(2):
                                  b = 2 * t + half
                                  P.mm(bank(b), gT[:, fc, t * 128:(t + 1) * 128], wd_sb[i_][:, f2, half * 512:(half + 1) * 512],
                                       start=(fc == 0), stop=(fc == NFC - 1), r=[RgT[fc], Rwd_sb[i_]], w=[Rps[b]])
                  for t in range(4):
                      T = 4 * g + t
                      for half in range(2):
                          b = 2 * t + half
                          P.tt("dve", x1[:, t, half * 512:(half + 1) * 512], x1[:, t, half * 512:(half + 1) * 512], bank(b), ALU.add,
                               r=[Rx1[t], Rps[b]], w=[Rx1[t]])
                      P.dma("pool", out[s, tsl(T), :], x1[:, t, :], r=[Rx1[t]], w=[P.reg()])
              assert not any(side_b), "head work left over"
        except _Stop:
            pass
        P.finish()
        build.stats = P.stats
    return nc


def host_consts():
    bf = ml_dtypes.bfloat16
    ident = np.eye(128, dtype=np.float32).astype(bf)
    kk = np.arange(128)[:, None]
    mm_ = np.arange(128)[None, :]
    tri = (kk <= mm_).astype(np.float32)
    sel = np.zeros((128, 128), np.float32)
    sel[127, :] = 1.0
    half = 8
    inv_freq = np.power(np.float32(ROPE_THETA), -np.arange(half, dtype=np.float32) * np.float32(2.0) / np.float32(16)).astype(np.float32)
    pos = (np.arange(NT)[None, :] * 128 + np.arange(128)[:, None]).astype(np.float32)
    ang = (pos[:, :, None] * inv_freq[None, None, :]).astype(np.float32)
    cos = np.cos(ang.astype(np.float64)).astype(np.float32).reshape(128, NT * 8)
    sin = np.sin(ang.astype(np.float64)).astype(np.float32).reshape(128, NT * 8)
    causal = (mm_ >= kk).astype(np.float32)
    band = (mm_ <= kk).astype(np.float32)
    mcb = np.concatenate([causal, band], axis=1).astype(bf)
    mc4 = np.concatenate([causal] * 4, axis=1).astype(bf)
    nn = np.arange(128)
    perm4 = np.zeros((128, 128), np.float32)
    perm4[4 * (nn % 32) + nn // 32, nn] = 1.0
    perm16 = np.zeros((128, 128), np.float32)
    perm16[16 * (nn % 8) + nn // 8, nn] = 1.0
    mbc = np.concatenate([band, causal], axis=1).astype(bf)
    pp = np.arange(128)[:, None, None, None]
    gg = np.arange(4)[None, :, None, None]
    ii = np.arange(32)[None, None, None, :]
    m16 = np.broadcast_to((pp <= 32 * gg + ii), (128, 4, 16, 32)).astype(np.float32).reshape(128, 4 * 512).astype(bf)
    return {"c_mbc": mbc, "c_m16": m16, "c_perm4": perm4.astype(bf), "c_perm16": perm16.astype(bf), "c_ident": ident, "c_tri": tri, "c_sel": sel, "c_cos": cos, "c_sin": sin, "c_mcb": mcb, "c_mc4": mc4}


_PARAMS = ("g_mix", "w_in", "b_forget", "g_q_fox", "g_k_fox", "g_q_dil", "g_k_dil", "g_out_fox", "g_out_dil",
           "w_out", "g_ffn", "w_gate", "w_up", "w_down")


def make_in_maps(inputs, nseq, ncores):
    consts = host_consts()
    shared = dict(consts)
    for k in _PARAMS:
        a = np.ascontiguousarray(np.asarray(inputs[k], dtype=np.float32))
        a = a.reshape(a.shape[1:]) if a.ndim == 3 else a.reshape(1, -1)
        shared[k] = a
    x = np.asarray(inputs["x"], dtype=np.float32)
    maps = []
    for c in range(ncores):
        m = dict(shared)
        m["x"] = np.ascontiguousarray(x[c * nseq:(c + 1) * nseq])
        maps.append(m)
    return maps


def kernel(**inputs):
    nseq = 4
    nc = build(nseq)
    in_maps = make_in_maps(inputs, nseq, NCORES)
    res = run_bass_kernel_spmd(nc, in_maps, core_ids=list(range(NCORES)))
    return np.concatenate([np.asarray(r["out"]) for r in res.results], axis=0).astype(np.float32)
```

```python
import contextlib
import math
import numpy as np
import ml_dtypes
import concourse.bass as bass
import concourse.mybir as mybir
from concourse.bass_utils import run_bass_kernel_spmd

F32 = mybir.dt.float32
BF16 = mybir.dt.bfloat16
ALU = mybir.AluOpType
AF = mybir.ActivationFunctionType
AX = mybir.AxisListType

D = 1024
KC = 8
S = 2048
NT = 16
HD = 64
DFF = 2816
NFC = 22
INC = 3080
EPS = 1e-6
NCORES = 8
ROPE_THETA = 500000.0
ROPE_ENG = "pool"
D4_ENG = "act"
PIPE_LOOK = 2
HEAD_DMA_Q = "act"
SIDE_FRAC = 0.8
FOX_K = 66
BOUNCE_Q = "sp"
PREP_STRIDE = 3
PREP_SPLIT = False
VCOPY_ENG = "dve"
DIL_MASK_ENG = lambda pi: "dve"
SIDE_FIRST = False
PIPE_DEFER = 3


class _Stop(Exception):
    pass


class Reg:
    __slots__ = ("name", "lw", "rd", "excl")

    def __init__(self, name):
        self.name = name
        self.lw = None
        self.rd = []
        self.excl = False


class Op:
    __slots__ = ("eng", "fn", "deps", "dma", "sem", "val", "signal", "cnt", "waits", "idx")

    def __init__(self, eng, fn, deps, dma, idx):
        self.eng = eng
        self.fn = fn
        self.deps = deps
        self.dma = dma
        self.sem = None
        self.val = 0
        self.signal = False
        self.cnt = 0
        self.waits = []
        self.idx = idx


ENGS = ("pe", "act", "dve", "pool", "sp")
BLOCKNAME = {"pe": "tensor", "act": "scalar", "dve": "vector", "pool": "gpsimd", "sp": "sync"}


class Prog:
    def __init__(self, nc, es, dma_pool=8):
        self.nc = nc
        self.es = es
        self.ops = []
        self.dma_pool = dma_pool
        self.nreg = 0

    def sb(self, name, shape, dt):
        return self.es.enter_context(self.nc.sbuf_tensor(name, list(shape), dt))

    def ps(self, name, shape, dt):
        return self.es.enter_context(self.nc.psum_tensor(name, list(shape), dt))

    def reg(self, name=None):
        self.nreg += 1
        return Reg(name or f"r{self.nreg}")

    def regs(self, n, name="r"):
        return [self.reg(f"{name}{i}") for i in range(n)]

    max_ops = None

    def op(self, eng, fn, r=(), w=(), dma=False):
        i = len(self.ops)
        if Prog.max_ops is not None and i >= Prog.max_ops:
            raise _Stop()
        if any(g.excl for g in r):
            w = list(w) + [g for g in r if g.excl]
            r = [g for g in r if not g.excl]
        deps = {}
        for g in r:
            if g.lw is not None:
                deps[g.lw] = True
        for g in w:
            if g.lw is not None:
                deps.setdefault(g.lw, False)
            for j in g.rd:
                deps.setdefault(j, False)
        for g in r:
            g.rd.append(i)
        for g in w:
            g.lw = i
            g.rd = []
        self.ops.append(Op(eng, fn, deps, dma, i))
        return i

    def handoff(self, old, new):
        allops = set()
        for g in old:
            if g.lw is not None:
                allops.add(g.lw)
            allops.update(g.rd)
        lst = sorted(allops)
        for g in new:
            g.rd = list(set(g.rd) | set(lst))

    def dma(self, q, out, in_, r=(), w=()):
        return self.op(q, lambda e: e.dma_start(out=out, in_=in_), r=r, w=w, dma=True)

    def mm(self, out, lhsT, rhs, start=True, stop=True, r=(), w=()):
        return self.op("pe", lambda e: e.matmul(out, lhsT=lhsT, rhs=rhs, start=start, stop=stop), r=r, w=w)

    def act(self, out, in_, func, r=(), w=(), **kw):
        return self.op("act", lambda e: e.activation(out=out, in_=in_, func=func, **kw), r=r, w=w)

    def tt(self, eng, out, in0, in1, op, r=(), w=()):
        return self.op(eng, lambda e: e.tensor_tensor(out=out, in0=in0, in1=in1, op=op), r=r, w=w)

    def stt(self, eng, out, in0, scalar, in1, op0, op1, r=(), w=()):
        return self.op(eng, lambda e: e.scalar_tensor_tensor(out=out, in0=in0, scalar=scalar, in1=in1, op0=op0, op1=op1),
                       r=r, w=w)

    def cp(self, eng, out, in_, r=(), w=()):
        if eng == "act":
            return self.op("act", lambda e: e.activation(out=out, in_=in_, func=AF.Copy), r=r, w=w)
        return self.op(eng, lambda e: e.tensor_copy(out=out, in_=in_), r=r, w=w)

    def finish(self):
        nc = self.nc
        ops = self.ops
        LIM = 16000
        sem_eng_l = {e: [] for e in ENGS}

        def sem_for(e, cnt):
            ep = (cnt - 1) // LIM
            while len(sem_eng_l[e]) <= ep:
                sem_eng_l[e].append(self.es.enter_context(nc.semaphore(f"s_{e}{len(sem_eng_l[e])}")))
            return sem_eng_l[e][ep], (cnt - 1) % LIM + 1
        dma_sems = {e: [self.es.enter_context(nc.semaphore(f"d_{e}{k}")) for k in range(self.dma_pool)]
                    for e in ("sp", "act", "pool")}
        dcount = {e: 0 for e in dma_sems}
        prev_on_sem = {}
        for o in ops:
            if o.dma:
                k = dcount[o.eng]
                dcount[o.eng] += 1
                slot = k % self.dma_pool
                o.sem = dma_sems[o.eng][slot]
                o.val = 16 * (k // self.dma_pool + 1)
                key = (o.eng, slot)
                if key in prev_on_sem:
                    o.deps.setdefault(prev_on_sem[key], False)
                prev_on_sem[key] = o.idx
        seen_eng = {e: {p: -1 for p in ENGS} for e in ENGS}
        seen_dma = {e: {} for e in ENGS}
        need = []
        for o in ops:
            e = o.eng
            best = {}
            dwaits = {}
            for d, is_raw in o.deps.items():
                Dp = ops[d]
                if Dp.dma:
                    key = id(Dp.sem)
                    if seen_dma[e].get(key, 0) >= Dp.val:
                        continue
                    if key not in dwaits or dwaits[key][1] < Dp.val:
                        dwaits[key] = (Dp.sem, Dp.val)
                else:
                    p = Dp.eng
                    if p == e and (e == "pe" or not is_raw):
                        continue
                    if d <= seen_eng[e][p]:
                        continue
                    if p not in best or best[p] < d:
                        best[p] = d
            o.waits = []
            for key, (sem, val) in dwaits.items():
                seen_dma[e][key] = val
                o.waits.append((sem, val))
            prods = []
            for p, d in best.items():
                seen_eng[e][p] = d
                ops[d].signal = True
                prods.append(d)
            need.append(prods)
        run = {e: 0 for e in ENGS}
        for o in ops:
            if o.dma:
                continue
            if o.signal:
                run[o.eng] += 1
            o.cnt = run[o.eng]
        for o, prods in zip(ops, need):
            for d in prods:
                Dp = ops[d]
                o.waits.append(sem_for(Dp.eng, Dp.cnt))
        final_waits = []
        for key, idx in prev_on_sem.items():
            Dp = ops[idx]
            final_waits.append((Dp.sem, Dp.val))
        by_eng = {e: [o for o in ops if o.eng == e] for e in ENGS}
        self.stats = {e: len(by_eng[e]) for e in ENGS}
        self.stats["signals"] = sum(1 for o in ops if o.signal)
        self.stats["waits"] = sum(len(o.waits) for o in ops)
        with nc.Block() as block:
            for e in ENGS:
                lst = by_eng[e]

                def f(eng, lst=lst, e=e):
                    for o in lst:
                        for sem, val in o.waits:
                            eng.wait_ge(sem, val)
                        ins = o.fn(eng)
                        if o.dma:
                            ins.then_inc(o.sem, 16)
                        elif o.signal:
                            ins.then_inc(sem_for(e, o.cnt)[0], 1)
                    if e == "sp":
                        for sem, val in final_waits:
                            eng.wait_ge(sem, val)

                getattr(block, BLOCKNAME[e])(f)


def _v(ap, shape):
    if len(shape) == 1:
        return ap
    if len(shape) == 2:
        return ap.rearrange("p (a b) -> p a b", a=shape[0])
    if len(shape) == 3:
        return ap.rearrange("p (a b c) -> p a b c", a=shape[0], b=shape[1])
    raise ValueError


def build(nseq=4, dbg=None, stop_after=None):
    nc = bass.Bass("TRN2", target_bir_lowering=False)
    build.dumps = []

    def chk(stage):
        if stop_after == stage:
            raise _Stop()

    def din(name, shape, dt=F32):
        return nc.dram_tensor(name, list(shape), dt, kind="ExternalInput").ap()

    def dscr(name, shape, dt=BF16):
        return nc.dram_tensor(name, list(shape), dt, kind="Internal").ap()

    x = din("x", [nseq, S, D])
    g_mix = din("g_mix", [1, D])
    w_in = din("w_in", [D, INC])
    b_forget = din("b_forget", [1, 8])
    g_q_fox = din("g_q_fox", [1, HD])
    g_k_fox = din("g_k_fox", [1, HD])
    g_q_dil = din("g_q_dil", [1, HD])
    g_k_dil = din("g_k_dil", [1, HD])
    g_out_fox = din("g_out_fox", [1, 512])
    g_out_dil = din("g_out_dil", [1, 512])
    w_out = din("w_out", [D, D])
    g_ffn = din("g_ffn", [1, D])
    w_gate = din("w_gate", [D, DFF])
    w_up = din("w_up", [D, DFF])
    w_down = din("w_down", [DFF, D])
    c_ident = din("c_ident", [128, 128], BF16)
    c_perm4 = din("c_perm4", [128, 128], BF16)
    c_perm16 = din("c_perm16", [128, 128], BF16)
    c_tri = din("c_tri", [128, 128])
    c_sel = din("c_sel", [128, 128])
    c_cos = din("c_cos", [128, NT * 8])
    c_sin = din("c_sin", [128, NT * 8])
    c_mcb = din("c_mcb", [128, 256], BF16)
    c_mc4 = din("c_mc4", [128, 512], BF16)
    c_mbc = din("c_mbc", [128, 256], BF16)
    c_m16 = din("c_m16", [128, 4 * 512], BF16)
    out = nc.dram_tensor("out", [nseq, S, D], F32, kind="ExternalOutput").ap()
    dbg_out = {}

    winb = dscr("winb", [D, 8, 384])
    wfb = dscr("wfb", [D, 8])
    woutb = dscr("woutb", [D, D])
    wgb = dscr("wgb", [D, DFF])
    wub = dscr("wub", [D, DFF])
    wdb = dscr("wdb", [DFF, D])
    vscr = [dscr(f"vscr{i}", [S, 130]) for i in range(2)]

    with contextlib.ExitStack() as es:
        P = Prog(nc, es)
        Rdump = P.reg("dump")

        def dump(name, ap, r):
            if not dbg or name not in dbg:
                return
            t_ = nc.dram_tensor(name, list(ap.shape), ap.dtype, kind="ExternalOutput").ap()
            build.dumps.append(name)
            P.dma("sp", t_, ap, r=r, w=[Rdump])
        ident = P.sb("ident", [128, 128], BF16)
        tri = P.sb("tri", [128, 128], F32)
        sel = P.sb("sel", [128, 128], F32)
        cos_t = P.sb("cos_t", [128, NT, 8], F32)
        sin_t = P.sb("sin_t", [128, NT, 8], F32)
        mcb = P.sb("mcb", [128, 256], BF16)
        mbc = P.sb("mbc", [128, 256], BF16)
        mC4 = P.sb("mC4", [128, 512], BF16)
        mCB2 = P.sb("mCB2", [128, 512], BF16)
        mBC2 = P.sb("mBC2", [128, 512], BF16)
        zeros_t = P.sb("zeros_t", [128, 512], BF16)
        m16 = P.sb("m16", [128, 4, 512], BF16)
        gmix_b = P.sb("gmix_b", [128, D], F32)
        gffn_b = P.sb("gffn_b", [128, D], F32)
        gout_b = P.sb("gout_b", [128, D], F32)
        gd_b = P.sb("gd_b", [128, 4, HD], F32)
        gfc = P.sb("gfc", [128, 2], F32)
        gs_f = P.sb("gs_f", [128, 1], F32)
        bf_b = P.sb("bf_b", [128, 8], F32)
        Rc = P.reg("consts")
        Rcl = []

        def cnew():
            Rcl.append(P.reg())
            return Rcl[-1]
        P.dma("sp", ident[:], c_ident, w=[cnew()])
        P.dma("sp", tri[:], c_tri, w=[cnew()])
        P.dma("sp", sel[:], c_sel, w=[cnew()])
        P.dma("sp", cos_t[:].rearrange("p a b -> p (a b)"), c_cos, w=[cnew()])
        P.dma("sp", sin_t[:].rearrange("p a b -> p (a b)"), c_sin, w=[cnew()])
        P.dma("sp", mcb[:], c_mcb, w=[cnew()])
        P.dma("sp", mbc[:], c_mbc, w=[cnew()])
        for h_ in range(2):
            P.dma("sp", mC4[:, h_ * 128:(h_ + 1) * 128], c_mcb[:, 0:128], w=[cnew()])
            P.dma("sp", mC4[:, (h_ + 2) * 128:(h_ + 3) * 128], c_mcb[:, 0:128], w=[cnew()])
            P.dma("sp", mCB2[:, h_ * 256:(h_ + 1) * 256], c_mcb, w=[cnew()])
            P.dma("sp", mBC2[:, h_ * 256:(h_ + 1) * 256], c_mbc, w=[cnew()])
        P.dma("sp", m16[:].rearrange("p a b -> p (a b)"), c_m16, w=[cnew()])
        P.dma("sp", gmix_b[:], g_mix.partition_broadcast(128), w=[cnew()])
        P.dma("sp", gffn_b[:], g_ffn.partition_broadcast(128), w=[cnew()])
        P.dma("sp", gout_b[:, 0:512], g_out_fox.partition_broadcast(128), w=[cnew()])
        P.dma("sp", gout_b[:, 512:1024], g_out_dil.partition_broadcast(128), w=[cnew()])
        P.dma("sp", gd_b[:, 0, :], g_q_dil.partition_broadcast(128), w=[cnew()])
        P.dma("sp", gd_b[:, 2, :], g_k_dil.partition_broadcast(128), w=[cnew()])
        P.dma("sp", bf_b[:], b_forget.partition_broadcast(128), w=[cnew()])
        P.dma("sp", gfc[0:64, 0:1], g_q_fox.rearrange("o e -> e o"), w=[cnew()])
        P.dma("sp", gfc[0:64, 1:2], g_k_fox.rearrange("o e -> e o"), w=[cnew()])
        P.op("dve", lambda e: e.tensor_copy(out=gd_b[:, 1, :], in_=gd_b[:, 0, :]), r=Rcl, w=[Rc])
        P.op("dve", lambda e: e.tensor_scalar_mul(out=gd_b[:, 2, :], in0=gd_b[:, 2, :], scalar1=8.0),
             r=[Rc], w=[Rc])
        P.op("dve", lambda e: e.tensor_copy(out=gd_b[:, 3, :], in_=gd_b[:, 2, :]), r=[Rc], w=[Rc])
        P.op("dve", lambda e: e.memset(gs_f[:], 1.0), w=[Rc])
        P.op("dve", lambda e: e.memset(zeros_t[:], 0.0), w=[Rc])
        P.stt("dve", gs_f[0:64, :], gfc[0:64, 0:1], 8.0, gfc[0:64, 1:2], ALU.mult, ALU.mult, r=[Rc], w=[Rc])

        Rwin = [[] for _ in range(8)]
        Rwf = P.reg("wfb")
        Rwo, Rwg, Rwu, Rwd = [], [], [], []
        def job_cols(j):
            p = j // 2
            if j % 2 == 0:
                return [1544 + 128 * p, 2056 + 128 * p, 2568 + 128 * p]
            return [128 * p, 512 + 128 * p, 1024 + 128 * p]
        P.dma("pool", wfb, w_in[:, 1536:1544], w=[Rwf])
        for j in range(8):
            for part, c0 in enumerate(job_cols(j)):
                Rwin[j].append(P.reg())
                P.dma("pool", winb[:, j, part * 128:(part + 1) * 128], w_in[:, c0:c0 + 128], w=[Rwin[j][-1]])
        pending_casts = []
        for i in range(KC):
            rs_ = slice(i * 128, (i + 1) * 128)
            pending_casts.append((woutb[rs_, :], w_out[rs_, :], Rwo))
        for i in range(KC):
            rs_ = slice(i * 128, (i + 1) * 128)
            pending_casts.append((wgb[rs_, :], w_gate[rs_, :], Rwg))
            pending_casts.append((wub[rs_, :], w_up[rs_, :], Rwu))
        for i in range(NFC):
            rs_ = slice(i * 128, (i + 1) * 128)
            pending_casts.append((wdb[rs_, :], w_down[rs_, :], Rwd))

        def emit_casts(n):
            for _ in range(min(n, len(pending_casts))):
                o_, i_, r_ = pending_casts.pop(0)
                r_.append(P.reg())
                P.dma("pool", o_, i_, w=[r_[-1]])

        xt = [P.sb(f"xt{i}", [128, D], F32) for i in range(2)]
        xn = [P.sb(f"xn{i}", [128, D], BF16) for i in range(2)]
        junk = P.sb("junk", [128, D], BF16)
        Rxt = P.regs(2, "xt")
        Rxn = P.regs(2, "xn")
        Rjunk = P.reg("junk")
        st1 = P.sb("st1", [128, 16], F32)
        Rst1 = P.regs(16, "st1")
        o_all = P.sb("o_all", [128, NT, D], BF16)
        Roall = P.regs(NT, "oall")
        cpt = P.sb("cpt", [128, NT, 8], F32)
        Rcp = P.regs(NT, "cp")
        mgb = P.sb("mgb", [128, 4, 8], F32)
        Rmgb = P.reg("mgb")
        biasT = P.sb("biasT", [128, 4, NT, 8], F32)
        Rbias = P.reg("bias")
        caug = P.sb("caug", [128, NT, 8], F32)
        caug_h32 = P.sb("caug_h32", [128, NT, 8], F32)
        caug_hl = P.sb("caug_hl", [128, NT, 8, 2], BF16)
        Rcaug = P.reg("caug")
        zt = P.sb("zt", [128, 2, 8], F32)
        Rzt = P.regs(2, "zt")
        pT = [P.sb(f"pT{i}", [128, 512], BF16) for i in range(4)]
        RpT = P.regs(4, "pT")
        oTf = [P.sb(f"oTf{i}", [128, 512], BF16) for i in range(2)]
        RoTf = P.regs(2, "oTf")
        sq = [P.sb(f"sq{i}", [128, 256], F32) for i in range(4)]
        Rsq = P.regs(4, "sq")
        ssr = [P.sb(f"ssr{i}", [128, 4], F32) for i in range(4)]
        Rssr = P.regs(4, "ssr")
        qn = [P.sb(f"qn{i}", [128, 4, HD], F32) for i in range(4)]
        Rqn = P.regs(4, "qn")
        rtmp = [P.sb(f"rtmp{i}", [128, 4, 4, 8], F32) for i in range(4)]
        Rrtmp = P.regs(4, "rtmp")
        qr = [P.sb(f"qr{i}", [128, 4, HD], BF16) for i in range(4)]
        Rqr = P.regs(4, "qr")
        qaug = [P.sb(f"qaug{i}", [128, 2, 66], BF16) for i in range(4)]
        kaug = [P.sb(f"kaug{i}", [128, 2, 66], BF16) for i in range(4)]
        Rqaug = P.regs(4, "qaug")
        Rkaug = P.regs(4, "kaug")
        sa_t = [P.sb(f"sa{i}", [128, 512], F32) for i in range(2)]
        Rsa = P.regs(2, "sa")
        rden = [P.sb(f"rden{i}", [128, 4], F32) for i in range(2)]
        Rrden = P.regs(2, "rden")
        for i in range(4):
            P.op("dve", lambda e, i=i: e.memset(kaug[i][:], 1.0), w=[Rkaug[i]])

        psum = P.ps("psum", [128, 8, 512], F32)
        Rps = P.regs(8, "bank")
        for g_ in Rps:
            g_.excl = True

        ARENA_B = 92160 - 16384
        arena = P.sb("arena", [128, ARENA_B // 2], BF16)
        hT = P.sb("hT", [128, KC, S], BF16)
        wf_sb = P.sb("wf_sb", [128, KC, 8], BF16)

        def carve(off, shape, dt=BF16):
            n = 1
            for s_ in shape:
                n *= s_
            if dt == BF16:
                v = arena[:, off // 2: off // 2 + n]
                nb = 2 * n
            else:
                v = arena[:, off // 2: off // 2 + 2 * n].bitcast(F32)
                nb = 4 * n
            return _v(v, shape), off + nb

        off = 0
        wj = []
        for i in range(2):
            t_, off = carve(off, [KC, 384])
            wj.append(t_)
        vd = []
        qkTd, off = carve(off, [2, S])
        for i in range(3):
            t_, off = carve(off, [NT, 2, 65])
            vd.append(t_)
        qTf, off = carve(off, [2, S])
        kTf, off = carve(off, [2, S])
        vf, off = carve(off, [NT, 2, 65])
        assert off <= ARENA_B, off
        off = 0
        x1, off = carve(off, [4, D], F32)
        h2T, off = carve(off, [KC, 512])
        gT, off = carve(off, [NFC, 512])
        wgu = []
        wo_sb, _ = carve(off, [KC, D])
        for i in range(2):
            t_, off = carve(off, [2, KC, 256])
            wgu.append(t_)
        wd_sb = []
        wgu_alt, _ = carve(off, [2, KC, 256])
        for i in range(2):
            t_, off = carve(off, [2, D])
            wd_sb.append(t_)
        onT = []
        for i in range(2):
            t_, off = carve(off, [KC, 128])
            onT.append(t_)
        assert off <= ARENA_B, off

        RhT = P.regs(NT, "hT")
        Rwj = P.regs(2, "wj")
        RqkTd = P.regs(NT, "qkTd")
        Rvd = [P.regs(NT, "vd0_"), P.regs(4, "vd1_"), [P.reg("vd2")]]
        RqTf = P.regs(NT, "qTf")
        RkTf = P.regs(NT, "kTf")
        Rvf = P.regs(NT, "vf")
        A_regs = Rwj + RqkTd + sum(Rvd, []) + RqTf + RkTf + Rvf
        Rwo_sb = P.reg("wo_sb")
        Rx1 = P.regs(4, "x1")
        Rh2T = P.regs(4, "h2T")
        RgT = P.regs(NFC, "gT")
        Rwgu2 = [P.regs(2, "wguA"), P.regs(2, "wguB")]
        Rwgu = Rwgu2[0] + Rwgu2[1]
        Rwd_sb = P.regs(2, "wd_sb")
        RonT = P.regs(2, "onT")
        BC_regs = Rx1 + Rh2T + RgT + Rwgu + Rwd_sb + RonT
        Rvscr = P.regs(2, "vscr")
        Rout = P.reg("out")
        Rxin = P.reg("xin")

        def bank(i, shape=None, parts=128):
            v = psum[0:parts, i, :]
            return v

        tsl = lambda T: slice(T * 128, (T + 1) * 128)

        def rstd_ops(col, n):
            c = st1[:, col:col + 1]
            P.act(c, c, AF.Ln, r=[Rst1[col]], w=[Rst1[col]], scale=1.0 / n, bias=EPS)
            P.act(c, c, AF.Exp, r=[Rst1[col]], w=[Rst1[col]], scale=-0.5)


        Rwf_sb = P.reg("wf_sb")
        P.dma("sp", wf_sb[:], wfb.rearrange("(c p) n -> p c n", p=128), r=[Rwf], w=[Rwf_sb])

        def head_stages(s, T):
            k = T % 2

            def a1():
                P.dma(HEAD_DMA_Q, xt[k][:], x[s, tsl(T), :], r=[Rxin], w=[Rxt[k]])

            def a2():
                P.act(junk[:], xt[k][:], AF.Square, r=[Rxt[k]], w=[Rjunk, Rst1[k]], accum_out=st1[:, k:k + 1])
                rstd_ops(k, D)

            def a3():
                P.stt("dve", xn[k][:], xt[k][:], st1[:, k:k + 1], gmix_b[:], ALU.mult, ALU.mult,
                      r=[Rxt[k], Rst1[k], Rc], w=[Rxn[k]])

            def a4():
                for half in range(2):
                    b = 4 + half
                    bv = _v(bank(b), [4, 128])
                    for c in range(4):
                        kc = half * 4 + c
                        P.mm(bv[:, c, :], xn[k][:, kc * 128:(kc + 1) * 128], ident[:], r=[Rxn[k], Rc], w=[Rps[b]])
                    P.cp("act" if half == 0 else "dve", hT[:, half * 4:(half + 1) * 4, tsl(T)], bv,
                         r=[Rps[b]], w=[RhT[T]])

            def g1():
                pf = bank(6)[:, 0:8]
                for kc in range(KC):
                    P.mm(pf, hT[:, kc, tsl(T)], wf_sb[:, kc, :], start=(kc == 0), stop=(kc == KC - 1),
                         r=[RhT[T], Rwf_sb], w=[Rps[6]])
                P.tt("dve", zt[:, k, :], pf, bf_b[:], ALU.add, r=[Rps[6], Rc], w=[Rzt[k]])

            def g2():
                P.act(zt[:, k, :], zt[:, k, :], AF.Exp, r=[Rzt[k]], w=[Rzt[k]], scale=-1.0)
                P.act(zt[:, k, :], zt[:, k, :], AF.Ln, r=[Rzt[k]], w=[Rzt[k]], bias=1.0, scale=1.0)

            def g3():
                pc = bank(7)[:, 0:8]
                P.mm(pc, tri[:], zt[:, k, :], start=True, stop=(T == 0), r=[Rc, Rzt[k]], w=[Rps[7]])
                if T > 0:
                    P.mm(pc, sel[:], cpt[:, T - 1, :], start=False, stop=True, r=[Rc, Rcp[T - 1]], w=[Rps[7]])
                P.cp("dve", cpt[:, T, :], pc, r=[Rps[7]], w=[Rcp[T]])
            return [a1, a2, a3, a4, g1, g2, g3]

        def gate_fin(s):
            P.op("dve", lambda e: e.memset(mgb[:, 0, :], 0.0), w=[Rmgb])
            for G in range(1, 4):
                pm = bank(6)[:, 8 * G:8 * G + 8]
                P.mm(pm, sel[:], cpt[:, 4 * G - 1, :], r=[Rc, Rcp[4 * G - 1]], w=[Rps[6]])
                P.cp("dve", mgb[:, G, :], pm, r=[Rps[6]], w=[Rmgb])
            for G in range(4):
                nj = 4 * G + 4
                P.tt("dve", biasT[:, G, 0:nj, :], cpt[:, 0:nj, :], mgb[:, G, :].unsqueeze(1).broadcast_to([128, nj, 8]),
                     ALU.subtract, r=Rcp + [Rmgb], w=[Rbias])
                P.tt("dve", caug[:, 4 * G:4 * G + 4, :], mgb[:, G, :].unsqueeze(1).broadcast_to([128, 4, 8]),
                     cpt[:, 4 * G:4 * G + 4, :], ALU.subtract, r=Rcp + [Rmgb], w=[Rcaug])
            P.cp("dve", caug_hl[:, :, :, 0], caug[:], r=[Rcaug], w=[Rcaug])
            P.cp("dve", caug_h32[:], caug_hl[:, :, :, 0], r=[Rcaug], w=[Rcaug])
            P.tt("dve", caug_h32[:], caug[:], caug_h32[:], ALU.subtract, r=[Rcaug], w=[Rcaug])
            P.cp("dve", caug_hl[:, :, :, 1], caug_h32[:], r=[Rcaug], w=[Rcaug])

        def head_batches(s):
            batches = []
            for b_ in range(4):
                tiles = list(range(5 * b_, min(NT, 5 * b_ + 5)))
                slots = {}
                for li, T in enumerate(tiles):
                    for si, f in enumerate(head_stages(s, T)):
                        slots.setdefault(li + si, []).append((si, f))
                lst = []
                for sl in sorted(slots):
                    fs = [f for _, f in sorted(slots[sl], key=lambda t: -t[0])]
                    lst.append(lambda fs=fs: [f() for f in fs])
                if b_ == 3:
                    lst.append(lambda: gate_fin(s))
                batches.append(lst)
            return batches

        try:
          chk("consts")
          for s in range(nseq):
              if s > 0:
                  P.handoff(BC_regs, A_regs)
              for i in range(3):
                  P.op("pool", lambda e, i=i: e.memset(vd[i][:, :, :, 64:65], 1.0), w=Rvd[i])
              P.op("pool", lambda e: e.memset(vf[:, :, :, 64:65], 1.0), w=Rvf)
              if FOX_K > 66:
                  P.op("pool", lambda e: e.memset(qTf[64:128, :, :], 0.0), w=RqTf)
                  P.op("pool", lambda e: e.memset(kTf[64:128, :, :], 0.0), w=RkTf)
              if s == 0:
                  for bl_ in head_batches(0):
                      for f_ in bl_:
                          f_()
              chk("A0")
              chk("gate")

              def prep_begin(j):
                  ws = j % 2
                  P.dma("sp", wj[ws][:], winb[:, j, :].rearrange("(c p) n -> p c n", p=128), r=Rwin[j], w=[Rwj[ws]])


              def prep_stages(j, T, pbanks=(0, 1)):
                  pair = j // 2
                  is_dil = (j % 2 == 0)
                  ws = j % 2
                  k = T % 4
                  pb = pbanks[T % len(pbanks)]
                  pj = bank(pb)[:, 0:384]
                  rs_b = ssr[k][:].unsqueeze(2).broadcast_to([128, 4, HD])
                  ptr = bank(2)

                  def s1a():
                      for kc in range(KC // 2):
                          P.mm(pj, hT[:, kc, tsl(T)], wj[ws][:, kc, :], start=(kc == 0), stop=False,
                               r=[RhT[T], Rwj[ws]], w=[Rps[pb]])

                  def s1():
                      if PREP_SPLIT:
                          rng = range(KC // 2, KC)
                      else:
                          rng = range(KC)
                      for kc in rng:
                          P.mm(pj, hT[:, kc, tsl(T)], wj[ws][:, kc, :], start=(kc == 0), stop=(kc == KC - 1),
                               r=[RhT[T], Rwj[ws]], w=[Rps[pb]])
                      P.act(sq[k][:], pj[:, 0:256], AF.Square, r=[Rps[pb]], w=[Rsq[k]])
                      if is_dil:
                          P.cp(VCOPY_ENG, vd[0][:, T, :, 0:64], _v(pj[:, 256:384], [2, HD]), r=[Rps[pb]], w=[Rvd[0][T]])
                      else:
                          P.cp(VCOPY_ENG, vf[:, T, :, 0:64], _v(pj[:, 256:384], [2, HD]), r=[Rps[pb]], w=[Rvf[T]])

                  def s2():
                      P.op("dve", lambda e: e.tensor_reduce(out=ssr[k][:], in_=_v(sq[k][:], [4, HD]), axis=AX.X, op=ALU.add),
                           r=[Rsq[k]], w=[Rssr[k]])

                  def s3():
                      P.act(ssr[k][:], ssr[k][:], AF.Ln, r=[Rssr[k]], w=[Rssr[k]], bias=HD * EPS, scale=1.0)
                      P.act(ssr[k][:], ssr[k][:], AF.Exp, r=[Rssr[k]], w=[Rssr[k]], scale=-0.5)

                  def s4():
                      if is_dil:
                          P.tt("dve", qn[k][:], _v(pj[:, 0:256], [4, HD]), rs_b, ALU.mult, r=[Rps[pb], Rssr[k]], w=[Rqn[k]])
                          P.tt("dve", qn[k][:], qn[k][:], gd_b[:], ALU.mult, r=[Rqn[k], Rc], w=[Rqn[k]])
                          P.cp("dve", qr[k][:, :, 16:64], qn[k][:, :, 16:64], r=[Rqn[k]], w=[Rqr[k]])
                      else:
                          P.tt("dve", qaug[k][:, :, 0:64], _v(pj[:, 0:128], [2, HD]), rs_b[:, 0:2, :], ALU.mult,
                               r=[Rps[pb], Rssr[k]], w=[Rqaug[k]])
                          P.tt("dve", kaug[k][:, :, 0:64], _v(pj[:, 128:256], [2, HD]), rs_b[:, 2:4, :], ALU.mult,
                               r=[Rps[pb], Rssr[k]], w=[Rkaug[k]])
                          P.cp("pool", qaug[k][:, :, 64:66], caug_hl[:, T, 2 * pair:2 * pair + 2, :], r=[Rcaug], w=[Rqaug[k]])

                  def s5():
                      if is_dil:
                          cb = cos_t[:, T, :].unsqueeze(1).broadcast_to([128, 4, 8])
                          sb_ = sin_t[:, T, :].unsqueeze(1).broadcast_to([128, 4, 8])
                          x1_ = qn[k][:, :, 0:8]
                          x2_ = qn[k][:, :, 8:16]
                          rt = rtmp[k]
                          P.tt(ROPE_ENG, rt[:, 0], x1_, cb, ALU.mult, r=[Rqn[k], Rc], w=[Rrtmp[k]])
                          P.tt(ROPE_ENG, rt[:, 1], x2_, sb_, ALU.mult, r=[Rqn[k], Rc], w=[Rrtmp[k]])
                          P.tt(ROPE_ENG, rt[:, 2], x2_, cb, ALU.mult, r=[Rqn[k], Rc], w=[Rrtmp[k]])
                          P.tt(ROPE_ENG, rt[:, 3], x1_, sb_, ALU.mult, r=[Rqn[k], Rc], w=[Rrtmp[k]])
                          P.tt(ROPE_ENG, qr[k][:, :, 0:8], rt[:, 0], rt[:, 1], ALU.subtract, r=[Rrtmp[k]], w=[Rqr[k]])
                          P.tt(ROPE_ENG, qr[k][:, :, 8:16], rt[:, 2], rt[:, 3], ALU.add, r=[Rrtmp[k]], w=[Rqr[k]])

                  def s6():
                      if is_dil:
                          ptv = _v(ptr[:, 0:256], [2, 128])
                          for i in range(2):
                              P.mm(ptv[:, i, :], qr[k][:, 2 * i:2 * i + 2, :].rearrange("p a b -> p (a b)"), ident[:],
                                   r=[Rqr[k], Rc], w=[Rps[2]])
                      else:
                          ptv = _v(ptr[0:66, :], [4, 128])
                          for i in range(2):
                              P.mm(ptv[:, i, :], qaug[k][:, i, :], ident[:], r=[Rqaug[k], Rc], w=[Rps[2]])
                          for i in range(2):
                              P.mm(ptv[:, 2 + i, :], kaug[k][:, i, :], ident[:], r=[Rkaug[k], Rc], w=[Rps[2]])

                  def s7():
                      if is_dil:
                          ptv = _v(ptr[:, 0:256], [2, 128])
                          P.cp("dve", qkTd[:, :, tsl(T)], ptv, r=[Rps[2]], w=[RqkTd[T]])
                      else:
                          ptv = _v(ptr[0:66, :], [4, 128])
                          P.op("dve", lambda e: e.tensor_scalar_mul(out=qTf[0:66, :, tsl(T)], in0=ptv[:, 0:2, :], scalar1=gs_f[0:66, 0:1]),
                               r=[Rps[2], Rc], w=[RqTf[T]])
                          P.cp("dve", kTf[0:66, :, tsl(T)], ptv[:, 2:4, :], r=[Rps[2]], w=[RkTf[T]])
                  def s67():
                      s6()
                      s7()
                  return ([s1a] if PREP_SPLIT else []) + [s1, s2, s3, s4, s5, s67]

              def prep_list(j, stride=PREP_STRIDE, pbanks=(0, 1)):
                  slots = {}
                  for T in range(NT):
                      for si, f in enumerate(prep_stages(j, T, pbanks)):
                          slots.setdefault(T * stride + si, []).append((si, f))
                  lst = [lambda: prep_begin(j)]
                  for sl in sorted(slots):
                      fs = [f for _, f in sorted(slots[sl], key=lambda t: -t[0])]
                      lst.append(lambda fs=fs: [f() for f in fs])
                  lst.append(lambda: prep_end(j))
                  return lst

              def prep_end(j):
                  if j % 2 == 0:
                      vs = (j // 2) % 2
                      P.dma(BOUNCE_Q, vscr[vs].rearrange("(t p) c -> p t c", p=128), vd[0][:].rearrange("p t a b -> p t (a b)"),
                            r=Rvd[0], w=[Rvscr[vs]])
                      for m_ in range(4):
                          P.dma(BOUNCE_Q, vd[1][:].rearrange("p (r m) a b -> p m r (a b)", r=4)[:, m_],
                                vscr[vs].rearrange("(m p r) c -> p m r c", m=4, r=4)[:, m_], r=[Rvscr[vs]], w=[Rvd[1][m_]])
                      P.dma(BOUNCE_Q, vd[2][:].rearrange("p r a b -> p r (a b)"),
                            vscr[vs].rearrange("(p r) c -> p r c", r=16), r=[Rvscr[vs]], w=Rvd[2])

              pipe_ctr = {"s": 0, "p": 0, "o": 0}

              def s_bank():
                  b_ = 3 + pipe_ctr["s"] % 3
                  pipe_ctr["s"] += 1
                  return b_

              def p_slot():
                  i_ = pipe_ctr["p"] % 4
                  pipe_ctr["p"] += 1
                  return i_

              def o_bank():
                  b_ = 6 + pipe_ctr["o"] % 2
                  pipe_ctr["o"] += 1
                  return b_

              def attn_steps(j):
                  pair = j // 2
                  is_dil = (j % 2 == 0)
                  steps = []
                  if not is_dil:
                      for hh in range(2):
                          h = 2 * pair + hh
                          for G in range(4):
                              nJ = 4 * G + 4
                              grp = {}
                              for J in range(nJ):
                                  def fnA(J=J, G=G, hh=hh, h=h, grp=grp):
                                      if J == 0:
                                          grp["ob"] = o_bank()
                                      o0 = max(0, (J - 4 * G) * 128)
                                      sb_i = s_bank()
                                      sv = bank(sb_i)
                                      P.mm(sv[:, o0:512], kTf[0:FOX_K, hh, tsl(J)], qTf[0:FOX_K, hh, G * 512 + o0:(G + 1) * 512],
                                           r=[RkTf[J]] + RqTf[4 * G:4 * G + 4], w=[Rps[sb_i]])
                                      pi = p_slot()
                                      grp[J] = pi
                                      P.act(pT[pi][:, o0:512], sv[:, o0:512], AF.Exp, r=[Rps[sb_i], Rbias], w=[RpT[pi]],
                                            bias=biasT[:, G, J, h:h + 1], scale=1.0)
                                      if J >= 4 * G:
                                          P.op("pool", lambda e, pi=pi, o0=o0: e.affine_select(
                                              out=pT[pi][:, o0:o0 + 128], in_=pT[pi][:, o0:o0 + 128], pattern=[[1, 128]],
                                              compare_op=ALU.is_ge, fill=0.0, base=0, channel_multiplier=-1),
                                              r=[RpT[pi]], w=[RpT[pi]])

                                  def fnB(J=J, G=G, hh=hh, h=h, grp=grp, nJ=nJ):
                                      o0 = max(0, (J - 4 * G) * 128)
                                      ob = grp["ob"]
                                      ov = bank(ob)[0:65, :]
                                      pi = grp[J]
                                      P.mm(ov[:, o0:512], vf[:, J, hh, :], pT[pi][:, o0:512], start=(J == 0), stop=(J == nJ - 1),
                                           r=[Rvf[J], RpT[pi]], w=[Rps[ob]])
                                      if J != nJ - 1:
                                          return None
                                      ei = G % 2
                                      P.cp("dve", oTf[ei][0:65, :], ov, r=[Rps[ob]], w=[RoTf[ei]])

                                      def fin():
                                          tv = _v(bank(2)[:, 0:260], [4, 65])
                                          for t in range(4):
                                              P.mm(tv[:, t, :], oTf[ei][0:65, t * 128:(t + 1) * 128], ident[0:65, 0:65],
                                                   r=[RoTf[ei], Rc], w=[Rps[2]])
                                          P.op("dve", lambda e: e.reciprocal(out=rden[ei][:], in_=tv[:, :, 64]),
                                               r=[Rps[2]], w=[Rrden[ei]])
                                          P.tt("dve", o_all[:, 4 * G:4 * G + 4, h * 64:(h + 1) * 64], tv[:, :, 0:64],
                                               rden[ei][:].unsqueeze(2).broadcast_to([128, 4, 64]), ALU.mult,
                                               r=[Rps[2], Rrden[ei]], w=Roall[4 * G:4 * G + 4])
                                      return [fin]
                                  steps.append((fnA, fnB))
                  else:
                      for hh in range(2):
                          h = 2 * pair + hh
                          rows = slice(64 * hh, 64 * hh + 64)
                          for G in range(4):
                              grp = {"n": 0}

                              def mkstep(sblocks, width, mrows, mask, pvs, grp=grp, G=G, h=h):
                                  loc = {}
                                  idx = grp["n"]
                                  grp["n"] += 1

                                  def fnA():
                                      if idx == 0:
                                          grp["ob"] = o_bank()
                                      sb_i = s_bank()
                                      sv = bank(sb_i)
                                      for lt, rh, c0, n, M in sblocks:
                                          P.mm(sv[0:M, c0:c0 + n], lt, rh, r=RqkTd, w=[Rps[sb_i]])
                                      pi = p_slot()
                                      loc["pi"] = pi
                                      P.act(pT[pi][0:mrows, 0:width], sv[0:mrows, 0:width], AF.Exp, r=[Rps[sb_i]], w=[RpT[pi]])
                                      P.tt(DIL_MASK_ENG(pi), pT[pi][0:mrows, 0:width], pT[pi][0:mrows, 0:width], mask, ALU.mult,
                                           r=[RpT[pi], Rc], w=[RpT[pi]])

                                  def fnB():
                                      ob = grp["ob"]
                                      ov = bank(ob)[0:65, :]
                                      pi = loc["pi"]
                                      last = (idx == grp["n"] - 1)
                                      if idx == 0:
                                          P.mm(ov, zeros_t[0:1, 0:65], zeros_t[0:1, 0:512], start=True, stop=False,
                                               r=[Rc], w=[Rps[ob]])
                                      for ii, (vt, K, c0, n, osl, Rv) in enumerate(pvs):
                                          P.mm(ov[:, osl], vt, pT[pi][0:K, c0:c0 + n], start=False,
                                               stop=(last and ii == len(pvs) - 1), r=Rv + [RpT[pi]], w=[Rps[ob]])
                                      if not last:
                                          return None
                                      ei = G % 2
                                      P.cp("dve", oTf[ei][0:65, :], ov, r=[Rps[ob]], w=[RoTf[ei]])

                                      def fin():
                                          tv = _v(bank(2)[:, 0:260], [4, 65])
                                          for t in range(4):
                                              P.mm(tv[:, t, :], oTf[ei][0:65, t * 128:(t + 1) * 128], ident[0:65, 0:65],
                                                   r=[RoTf[ei], Rc], w=[Rps[2]])
                                          P.op("dve", lambda e: e.reciprocal(out=rden[ei][:], in_=tv[:, :, 64]),
                                               r=[Rps[2]], w=[Rrden[ei]])
                                          P.tt("dve", o_all[:, 4 * G:4 * G + 4, 512 + h * 64:512 + (h + 1) * 64], tv[:, :, 0:64],
                                               rden[ei][:].unsqueeze(2).broadcast_to([128, 4, 64]), ALU.mult,
                                               r=[Rps[2], Rrden[ei]], w=Roall[4 * G:4 * G + 4])
                                      return [fin]
                                  steps.append((fnA, fnB))

                              def d1_blk(J):
                                  if J == 4 * G - 1:
                                      return (J, 4 * G, 128)
                                  if J == 4 * G + 3:
                                      return (J, J, 128)
                                  return (J, J, 256)
                              if G >= 1:
                                  packs = [([4 * G - 1, 4 * G, 4 * G + 3], mBC2[:, 0:512]), ([4 * G + 1, 4 * G + 2], mCB2[:, 0:512])]
                              else:
                                  packs = [([0, 3], mCB2[:, 0:384]), ([1, 2], mCB2[:, 0:512])]
                              for Jl, mask in packs:
                                  sbl, pvs = [], []
                                  c0 = 0
                                  for J in Jl:
                                      J, qb, n = d1_blk(J)
                                      oc = (qb - 4 * G) * 128
                                      sbl.append((qkTd[rows, 1, tsl(J)], qkTd[rows, 0, qb * 128:qb * 128 + n], c0, n, 128))
                                      pvs.append((vd[0][:, J, hh, :], 128, c0, n, slice(oc, oc + n), [Rvd[0][J]]))
                                      c0 += n
                                  mkstep(sbl, c0, 128, mask, pvs)
                              ms = ([G - 1, G] if G >= 1 else [G])
                              rpacks = [[0, 1], [2, 3]] if G >= 1 else [[0, 1, 2, 3]]
                              for rl in rpacks:
                                  sbl, pvs = [], []
                                  c0 = 0
                                  for r4 in rl:
                                      qs = slice(512 * G + r4, 512 * (G + 1), 4)
                                      for m in ms:
                                          ks = slice(512 * m + r4, 512 * (m + 1), 4)
                                          sbl.append((qkTd[rows, 1, ks], qkTd[rows, 0, qs], c0, 128, 128))
                                          pvs.append((vd[1][:, r4 * 4 + m, hh, :], 128, c0, 128, slice(r4, 512, 4), [Rvd[1][m]]))
                                          c0 += 128
                                  mask = mBC2[:, 0:512] if G >= 1 else mC4[:, 0:512]
                                  mkstep(sbl, c0, 128, mask, pvs)
                              M = 32 * (G + 1)
                              sbl, pvs = [], []
                              for r16 in range(16):
                                  ks = slice(r16, min(r16 + 16 * M, S), 16)
                                  qs = slice(512 * G + r16, 512 * (G + 1), 16)
                                  sbl.append((qkTd[rows, 1, ks], qkTd[rows, 0, qs], 32 * r16, 32, M))
                                  pvs.append((vd[2][0:M, r16, hh, :], M, 32 * r16, 32, slice(r16, 512, 16), Rvd[2]))
                              mkstep(sbl, 512, M, m16[0:M, G, :], pvs)
                  return steps

              def run_pipeline(steps, side, look=PIPE_LOOK, defer=PIPE_DEFER):
                  n = len(steps)
                  pend = []
                  nside = len(side)
                  sidx = 0
                  if SIDE_FIRST:
                      for f_ in side:
                          f_()
                      sidx = nside
                  emitted = 0
                  for i in range(n):
                      while emitted < min(n, i + look + 1):
                          steps[emitted][0]()
                          emitted += 1
                      d = steps[i][1]()
                      pend = [(c - 1, f) for c, f in pend]
                      while pend and pend[0][0] <= 0:
                          pend.pop(0)[1]()
                      if d:
                          for f in d:
                              pend.append((defer, f))
                      want = min(nside, ((i + 1) * nside) // max(1, int(SIDE_FRAC * n)))
                      while sidx < want:
                          side[sidx]()
                          sidx += 1
                  for _, f in pend:
                      f()
                  while sidx < nside:
                      side[sidx]()
                      sidx += 1

              for f_ in prep_list(0, stride=1, pbanks=(0, 1, 3, 4)):
                  f_()
              chk("prep0")
              for j in range(8):
                  side = []
                  if j + 1 < 8:
                      side += prep_list(j + 1)
                  if s == 0:
                      side.append(lambda: emit_casts(6))
                  run_pipeline(attn_steps(j), side)
                  chk(f"attn{j}")

              if s == 0:
                  dump("d_oall", o_all[:], Roall)
              chk("attn")
              emit_casts(1000)
              P.handoff(A_regs, BC_regs)
              wctr = 0
              dctr = 0
              side_b = head_batches(s + 1) if s + 1 < nseq else [[], [], [], []]
              P.dma("sp", wo_sb, woutb.rearrange("(c p) n -> p c n", p=128), r=Rwo, w=Rwgu)
              for g in range(4):
                  P.dma("sp", wgu_alt[:, 0], wgb[:, 0:256].rearrange("(c p) n -> p c n", p=128), r=Rwg, w=[Rwd_sb[0]])
                  P.dma("sp", wgu_alt[:, 1], wub[:, 0:256].rearrange("(c p) n -> p c n", p=128), r=Rwu, w=[Rwd_sb[1]])
                  def pb_stages(t, g=g, s=s):
                      T = 4 * g + t
                      k = t % 2
                      c0 = 4 + 3 * t

                      def b1():
                          P.act(junk[:, 0:512], o_all[:, T, 0:512], AF.Square, r=[Roall[T]], w=[Rjunk, Rst1[c0]],
                                accum_out=st1[:, c0:c0 + 1])
                          P.act(junk[:, 512:1024], o_all[:, T, 512:1024], AF.Square, r=[Roall[T]], w=[Rjunk, Rst1[c0 + 1]],
                                accum_out=st1[:, c0 + 1:c0 + 2])

                      def b2():
                          rstd_ops(c0, 512)
                          rstd_ops(c0 + 1, 512)
                          P.dma(HEAD_DMA_Q, x1[:, t, :], x[s, tsl(T), :], r=[Rxin], w=[Rx1[t]])

                      def b3():
                          P.stt("dve", xn[k][:, 0:512], o_all[:, T, 0:512], st1[:, c0:c0 + 1], gout_b[:, 0:512],
                                ALU.mult, ALU.mult, r=[Roall[T], Rst1[c0], Rc], w=[Rxn[k]])
                          P.stt("dve", xn[k][:, 512:1024], o_all[:, T, 512:1024], st1[:, c0 + 1:c0 + 2], gout_b[:, 512:1024],
                                ALU.mult, ALU.mult, r=[Roall[T], Rst1[c0 + 1], Rc], w=[Rxn[k]])

                      def b4():
                          for half in range(2):
                              b = 2 * k + half
                              bv = _v(bank(b), [4, 128])
                              for c in range(4):
                                  kc = half * 4 + c
                                  P.mm(bv[:, c, :], xn[k][:, kc * 128:(kc + 1) * 128], ident[:], r=[Rxn[k], Rc], w=[Rps[b]])
                              P.cp("act" if half == 0 else "dve", onT[k][:, half * 4:(half + 1) * 4, :], bv, r=[Rps[b]], w=[RonT[k]])

                      def b5():
                          for half in range(2):
                              b = 4 + half
                              for kc in range(KC):
                                  P.mm(bank(b), onT[k][:, kc, :], wo_sb[:, kc, half * 512:(half + 1) * 512],
                                       start=(kc == 0), stop=(kc == KC - 1), r=[RonT[k]] + Rwgu, w=[Rps[b]])
                              P.tt("dve", x1[:, t, half * 512:(half + 1) * 512], x1[:, t, half * 512:(half + 1) * 512], bank(b), ALU.add,
                                   r=[Rx1[t], Rps[b]], w=[Rx1[t]])

                      def b6():
                          P.act(junk[:], x1[:, t, :], AF.Square, r=[Rx1[t]], w=[Rjunk, Rst1[c0 + 2]], accum_out=st1[:, c0 + 2:c0 + 3])
                          rstd_ops(c0 + 2, D)

                      def b7():
                          P.stt("dve", xn[k][:], x1[:, t, :], st1[:, c0 + 2:c0 + 3], gffn_b[:], ALU.mult, ALU.mult,
                                r=[Rx1[t], Rst1[c0 + 2], Rc], w=[Rxn[k]])

                      def b8():
                          for half in range(2):
                              b = 2 * k + half
                              bv = _v(bank(b), [4, 128])
                              for c in range(4):
                                  kc = half * 4 + c
                                  P.mm(bv[:, c, :], xn[k][:, kc * 128:(kc + 1) * 128], ident[:], r=[Rxn[k], Rc], w=[Rps[b]])
                              P.cp("act" if half == 0 else "dve", h2T[:, half * 4:(half + 1) * 4, t * 128:(t + 1) * 128], bv,
                                   r=[Rps[b]], w=[Rh2T[t]])
                      return [b1, b2, b3, b4, b5, b6, b7, b8]

                  pslots = {}
                  for t in range(4):
                      for si, f in enumerate(pb_stages(t)):
                          pslots.setdefault(t + si, []).append((si, f))
                  for sl in sorted(pslots):
                      for _, f in sorted(pslots[sl], key=lambda q_: -q_[0]):
                          f()
                  chk(f"B{g}")
                  for fp in range(NFC // 2):
                      csl = slice(fp * 256, (fp + 1) * 256)
                      if fp == 0:
                          wcur, Rwcur = wgu_alt, [Rwd_sb[0], Rwd_sb[1]]
                      else:
                          wsl = wctr % 2
                          wctr += 1
                          wcur, Rwcur = wgu[wsl], Rwgu2[wsl]
                          P.dma("sp", wcur[:, 0], wgb[:, csl].rearrange("(c p) n -> p c n", p=128), r=Rwg, w=[Rwcur[0]])
                          P.dma("sp", wcur[:, 1], wub[:, csl].rearrange("(c p) n -> p c n", p=128), r=Rwu, w=[Rwcur[1]])
                      for f2 in range(2):
                          fc = 2 * fp + f2
                          ba = 0 + 2 * (fc % 2)
                          bu = 1 + 2 * (fc % 2)
                          for which, bb in ((0, ba), (1, bu)):
                              for kc in range(KC):
                                  P.mm(bank(bb), wcur[:, which, kc, f2 * 128:(f2 + 1) * 128], h2T[:, kc, :],
                                       start=(kc == 0), stop=(kc == KC - 1), r=[Rwcur[which]] + Rh2T, w=[Rps[bb]])
                          si = fc % 2
                          sa = sa_t[si][:]
                          P.act(sa, bank(ba), AF.Silu, r=[Rps[ba]], w=[Rsa[si]])
                          P.tt("dve", gT[:, fc, :], sa, bank(bu), ALU.mult, r=[Rsa[si], Rps[bu]], w=[RgT[fc]])
                      if side_b[g]:
                          side_b[g].pop(0)()
                  chk(f"gu{g}")
                  for fp in range(NFC // 2):
                      i_ = dctr % 2
                      dctr += 1
                      P.dma("sp", wd_sb[i_][:], wdb[fp * 256:(fp + 1) * 256, :].rearrange("(a p) n -> p a n", p=128),
                            r=Rwd, w=[Rwd_sb[i_]])
                      if fp == 1 and g < 3:
                          P.dma("sp", wo_sb, woutb.rearrange("(c p) n -> p c n", p=128), r=Rwo, w=Rwgu)
                      for f2 in range(2):
                          fc = 2 * fp + f2
                          for t in range(4):
                              for half in range(2):
                                  b = 2 * t + half
                                  P.mm(bank(b), gT[:, fc, t * 128:(t + 1) * 128], wd_sb[i_][:, f2, half * 512:(half + 1) * 512],
                                       start=(fc == 0), stop=(fc == NFC - 1), r=[RgT[fc], Rwd_sb[i_]], w=[Rps[b]])
                  for t in range(4):
                      T = 4 * g + t
                      for half in range(2):
                          b = 2 * t + half
                          P.tt("dve", x1[:, t, half * 512:(half + 1) * 512], x1[:, t, half * 512:(half + 1) * 512], bank(b), ALU.add,
                               r=[Rx1[t], Rps[b]], w=[Rx1[t]])
                      P.dma("pool", out[s, tsl(T), :], x1[:, t, :], r=[Rx1[t]], w=[P.reg()])
              assert not any(side_b), "head work left over"
        except _Stop:
            pass
        P.finish()
        build.stats = P.stats
    return nc


def host_consts():
    bf = ml_dtypes.bfloat16
    ident = np.eye(128, dtype=np.float32).astype(bf)
    kk = np.arange(128)[:, None]
    mm_ = np.arange(128)[None, :]
    tri = (kk <= mm_).astype(np.float32)
    sel = np.zeros((128, 128), np.float32)
    sel[127, :] = 1.0
    half = 8
    inv_freq = np.power(np.float32(ROPE_THETA), -np.arange(half, dtype=np.float32) * np.float32(2.0) / np.float32(16)).astype(np.float32)
    pos = (np.arange(NT)[None, :] * 128 + np.arange(128)[:, None]).astype(np.float32)
    ang = (pos[:, :, None] * inv_freq[None, None, :]).astype(np.float32)
    cos = np.cos(ang.astype(np.float64)).astype(np.float32).reshape(128, NT * 8)
    sin = np.sin(ang.astype(np.float64)).astype(np.float32).reshape(128, NT * 8)
    causal = (mm_ >= kk).astype(np.float32)
    band = (mm_ <= kk).astype(np.float32)
    mcb = np.concatenate([causal, band], axis=1).astype(bf)
    mc4 = np.concatenate([causal] * 4, axis=1).astype(bf)
    nn = np.arange(128)
    perm4 = np.zeros((128, 128), np.float32)
    perm4[4 * (nn % 32) + nn // 32, nn] = 1.0
    perm16 = np.zeros((128, 128), np.float32)
    perm16[16 * (nn % 8) + nn // 8, nn] = 1.0
    mbc = np.concatenate([band, causal], axis=1).astype(bf)
    pp = np.arange(128)[:, None, None, None]
    gg = np.arange(4)[None, :, None, None]
    ii = np.arange(32)[None, None, None, :]
    m16 = np.broadcast_to((pp <= 32 * gg + ii), (128, 4, 16, 32)).astype(np.float32).reshape(128, 4 * 512).astype(bf)
    return {"c_mbc": mbc, "c_m16": m16, "c_perm4": perm4.astype(bf), "c_perm16": perm16.astype(bf), "c_ident": ident, "c_tri": tri, "c_sel": sel, "c_cos": cos, "c_sin": sin, "c_mcb": mcb, "c_mc4": mc4}


_PARAMS = ("g_mix", "w_in", "b_forget", "g_q_fox", "g_k_fox", "g_q_dil", "g_k_dil", "g_out_fox", "g_out_dil",
           "w_out", "g_ffn", "w_gate", "w_up", "w_down")


def make_in_maps(inputs, nseq, ncores):
    consts = host_consts()
    shared = dict(consts)
    for k in _PARAMS:
        a = np.ascontiguousarray(np.asarray(inputs[k], dtype=np.float32))
        a = a.reshape(a.shape[1:]) if a.ndim == 3 else a.reshape(1, -1)
        shared[k] = a
    x = np.asarray(inputs["x"], dtype=np.float32)
    maps = []
    for c in range(ncores):
        m = dict(shared)
        m["x"] = np.ascontiguousarray(x[c * nseq:(c + 1) * nseq])
        maps.append(m)
    return maps


def kernel(**inputs):
    nseq = 4
    nc = build(nseq)
    in_maps = make_in_maps(inputs, nseq, NCORES)
    res = run_bass_kernel_spmd(nc, in_maps, core_ids=list(range(NCORES)))
    return np.concatenate([np.asarray(r["out"]) for r in res.results], axis=0).astype(np.float32)
```
